# Optimizing a Trainium2 kernel written in Bass

```python
import jax, jax.numpy as jnp
from jax import lax
import numpy as np

D_MODEL = 1024
BATCH = 8
SEQ = 2048
DEPTH = 2

HEAD_DIM = 64
SWA_Q_HEADS = 8
SWA_KV_HEADS = 2
SWA_WINDOW = 128
MOBA_HEADS = 8
MOBA_BLOCK = 256
MOBA_TOPK = 3
MOBA_Q_CHUNK = 32
D_FF = 4 * D_MODEL
ROPE_THETA = 10000.0
NORM_EPS = 1e-6
NEG_INF = -1e30

SWA_Q_W = SWA_Q_HEADS * HEAD_DIM
SWA_KV_W = SWA_KV_HEADS * HEAD_DIM
MOBA_W = MOBA_HEADS * HEAD_DIM
IN_SPLITS = [SWA_Q_W, SWA_KV_W, SWA_KV_W, MOBA_W, MOBA_W, MOBA_W, D_MODEL, D_MODEL]
IN_W = int(sum(IN_SPLITS))
IN_OFFSETS = [int(o) for o in np.cumsum(IN_SPLITS)[:-1]]

kernel_name = "hybrid_swa_sink_moba_sqrelu_adaln"


def rms_norm(x, gain):
    xf = x.astype(jnp.float32)
    y = xf * lax.rsqrt(jnp.mean(xf * xf, axis=-1, keepdims=True) + NORM_EPS)
    return (y * gain.astype(jnp.float32)).astype(x.dtype)


def rope(x, positions):
    half = HEAD_DIM // 2
    inv_freq = ROPE_THETA ** (-jnp.arange(half, dtype=jnp.float32) / half)
    ang = positions.astype(jnp.float32)[..., None] * inv_freq
    cos = jnp.cos(ang)[:, :, None, :]
    sin = jnp.sin(ang)[:, :, None, :]
    xf = x.astype(jnp.float32)
    x1, x2 = xf[..., :half], xf[..., half:]
    out = jnp.concatenate([x1 * cos - x2 * sin, x2 * cos + x1 * sin], axis=-1)
    return out.astype(x.dtype)


def swa_attention(q, k, v, sinks):
    B, S = q.shape[:2]
    W = SWA_WINDOW
    nb = S // W
    G = SWA_Q_HEADS // SWA_KV_HEADS
    qb = q.reshape(B, nb, W, SWA_KV_HEADS, G, HEAD_DIM)

    def band(t):
        tb = t.reshape(B, nb, W, SWA_KV_HEADS, HEAD_DIM)
        prev = jnp.pad(tb[:, :-1], ((0, 0), (1, 0), (0, 0), (0, 0), (0, 0)))
        return jnp.concatenate([prev, tb], axis=2)

    kb, vb = band(k), band(v)
    s = jnp.einsum('bnqhgd,bnkhd->bnhgqk', qb, kb,
                   preferred_element_type=jnp.float32) * (HEAD_DIM ** -0.5)
    qi = jnp.arange(W)[:, None]
    kj = jnp.arange(2 * W)[None, :]
    diff = qi + W - kj
    blk = jnp.arange(nb)[:, None, None]
    valid = (diff >= 0) & (diff < W) & (blk * W - W + kj >= 0)
    s = jnp.where(valid[None, :, None, None], s, NEG_INF)
    sink = sinks.astype(jnp.float32).reshape(SWA_KV_HEADS, G)[None, None, :, :, None, None]
    sink = jnp.broadcast_to(sink, s.shape[:-1] + (1,))
    p = jax.nn.softmax(jnp.concatenate([s, sink], axis=-1), axis=-1)[..., :-1]
    o = jnp.einsum('bnhgqk,bnkhd->bnqhgd', p.astype(v.dtype), vb)
    return o.reshape(B, S, SWA_Q_W)


def _take_blocks(blocks, idx):
    return blocks[idx]


_gather_bh = jax.vmap(jax.vmap(_take_blocks))


def moba_attention(q, k, v):
    B, S = q.shape[:2]
    L = MOBA_BLOCK
    nblk = -(-S // L)
    s_pad = nblk * L
    pad = ((0, 0), (0, s_pad - S), (0, 0), (0, 0))
    kb = jnp.pad(k, pad).reshape(B, nblk, L, MOBA_HEADS, HEAD_DIM).transpose(0, 3, 1, 2, 4)
    vb = jnp.pad(v, pad).reshape(B, nblk, L, MOBA_HEADS, HEAD_DIM).transpose(0, 3, 1, 2, 4)
    k_mean = jnp.mean(kb.astype(jnp.float32), axis=3)
    qh = q.transpose(0, 2, 1, 3)
    gate = jnp.einsum('bhsd,bhnd->bhsn', qh.astype(jnp.float32), k_mean)
    qblk = jnp.arange(S) // L
    past = jnp.arange(nblk)[None, :] < qblk[:, None]
    gate = jnp.where(past[None, None], gate, NEG_INF)
    topk = min(MOBA_TOPK, nblk)
    _, sel = lax.top_k(gate, topk)
    sel_valid = sel < qblk[None, None, :, None]
    scale = HEAD_DIM ** -0.5
    QC = MOBA_Q_CHUNK
    n_chunks = S // QC

    def chunk(ci):
        start = ci * QC
        qc = lax.dynamic_slice_in_dim(qh, start, QC, axis=2)
        selc = lax.dynamic_slice_in_dim(sel, start, QC, axis=2)
        validc = lax.dynamic_slice_in_dim(sel_valid, start, QC, axis=2)
        kg = _gather_bh(kb, selc)
        vg = _gather_bh(vb, selc)
        s_sel = jnp.einsum('bhqd,bhqtld->bhqtl', qc, kg,
                           preferred_element_type=jnp.float32) * scale
        s_sel = jnp.where(validc[..., None], s_sel, NEG_INF).reshape(B, MOBA_HEADS, QC, topk * L)
        own = start // L
        k_own = lax.dynamic_index_in_dim(kb, own, axis=2, keepdims=False)
        v_own = lax.dynamic_index_in_dim(vb, own, axis=2, keepdims=False)
        s_own = jnp.einsum('bhqd,bhld->bhql', qc, k_own,
                           preferred_element_type=jnp.float32) * scale
        qpos = start + jnp.arange(QC)
        kpos = own * L + jnp.arange(L)
        s_own = jnp.where(kpos[None, :] <= qpos[:, None], s_own, NEG_INF)
        p = jax.nn.softmax(jnp.concatenate([s_sel, s_own], axis=-1), axis=-1).astype(v.dtype)
        p_sel = p[..., :topk * L].reshape(B, MOBA_HEADS, QC, topk, L)
        p_own = p[..., topk * L:]
        return (jnp.einsum('bhqtl,bhqtld->bhqd', p_sel, vg)
                + jnp.einsum('bhql,bhld->bhqd', p_own, v_own))

    out = lax.map(chunk, jnp.arange(n_chunks))
    return out.transpose(1, 0, 3, 2, 4).reshape(B, S, MOBA_W)


def adaln(h, shift, scale):
    return h * (1 + scale[:, None, :]) + shift[:, None, :]


def hybrid_layer(x, c, positions, g_mix, g_mlp, w_ada, b_ada, w_in, qn_swa, kn_swa,
                 qn_moba, kn_moba, sinks, w_o_swa, w_o_moba, w_out, w_up, w_down):
    B, S, _ = x.shape
    mod = jax.nn.silu(c) @ w_ada + b_ada
    sh1, sc1, gt1, sh2, sc2, gt2 = jnp.split(mod, 6, axis=-1)

    h = adaln(rms_norm(x, g_mix), sh1, sc1)
    proj = h @ w_in
    q_a, k_a, v_a, q_b, k_b, v_b, gate_a, gate_b = jnp.split(proj, IN_OFFSETS, axis=-1)
    q_a = rope(rms_norm(q_a.reshape(B, S, SWA_Q_HEADS, HEAD_DIM), qn_swa), positions)
    k_a = rope(rms_norm(k_a.reshape(B, S, SWA_KV_HEADS, HEAD_DIM), kn_swa), positions)
    v_a = v_a.reshape(B, S, SWA_KV_HEADS, HEAD_DIM)
    q_b = rope(rms_norm(q_b.reshape(B, S, MOBA_HEADS, HEAD_DIM), qn_moba), positions)
    k_b = rope(rms_norm(k_b.reshape(B, S, MOBA_HEADS, HEAD_DIM), kn_moba), positions)
    v_b = v_b.reshape(B, S, MOBA_HEADS, HEAD_DIM)
    y_a = swa_attention(q_a, k_a, v_a, sinks) @ w_o_swa
    y_b = moba_attention(q_b, k_b, v_b) @ w_o_moba
    mixed = jax.nn.sigmoid(gate_a) * y_a + jax.nn.sigmoid(gate_b) * y_b
    x = x + gt1[:, None, :] * (mixed @ w_out)

    h = adaln(rms_norm(x, g_mlp), sh2, sc2)
    u = jnp.square(jax.nn.relu(h @ w_up))
    x = x + gt2[:, None, :] * (u @ w_down)
    return x


def setup_inputs(seed: int = 0) -> dict:
    key = jax.random.key(seed)
    ks = jax.random.split(key, 20)
    f32 = jnp.float32

    def nrm(k, shape, scale):
        return jax.random.normal(k, shape, f32) * scale

    x = jax.random.normal(ks[0], (BATCH, SEQ, D_MODEL), f32)
    c = jax.random.normal(ks[1], (BATCH, D_MODEL), f32)
    offs = jax.random.randint(ks[2], (BATCH, 1), 0, 1024, dtype=jnp.int32)
    positions = (jnp.arange(SEQ, dtype=jnp.int32)[None, :] + offs).astype(jnp.int32)
    return {
        "x": x,
        "c": c,
        "positions": positions,
        "rms_mix": 1.0 + nrm(ks[3], (DEPTH, D_MODEL), 0.05),
        "rms_mlp": 1.0 + nrm(ks[4], (DEPTH, D_MODEL), 0.05),
        "w_ada": nrm(ks[5], (DEPTH, D_MODEL, 6 * D_MODEL), 0.5 * D_MODEL ** -0.5),
        "b_ada": nrm(ks[6], (DEPTH, 6 * D_MODEL), 0.02),
        "w_in": nrm(ks[7], (DEPTH, D_MODEL, IN_W), D_MODEL ** -0.5),
        "q_norm_swa": 1.0 + nrm(ks[8], (DEPTH, HEAD_DIM), 0.05),
        "k_norm_swa": 1.0 + nrm(ks[9], (DEPTH, HEAD_DIM), 0.05),
        "q_norm_moba": 1.0 + nrm(ks[10], (DEPTH, HEAD_DIM), 0.05),
        "k_norm_moba": 1.0 + nrm(ks[11], (DEPTH, HEAD_DIM), 0.05),
        "swa_sinks": nrm(ks[12], (DEPTH, SWA_Q_HEADS), 0.5),
        "w_o_swa": nrm(ks[13], (DEPTH, SWA_Q_W, D_MODEL), SWA_Q_W ** -0.5),
        "w_o_moba": nrm(ks[14], (DEPTH, MOBA_W, D_MODEL), MOBA_W ** -0.5),
        "w_out": nrm(ks[15], (DEPTH, D_MODEL, D_MODEL), D_MODEL ** -0.5),
        "w_up": nrm(ks[16], (DEPTH, D_MODEL, D_FF), D_MODEL ** -0.5),
        "w_down": nrm(ks[17], (DEPTH, D_FF, D_MODEL), D_FF ** -0.5),
    }


def reference(x, c, positions, rms_mix, rms_mlp, w_ada, b_ada, w_in, q_norm_swa, k_norm_swa,
              q_norm_moba, k_norm_moba, swa_sinks, w_o_swa, w_o_moba, w_out, w_up, w_down):
    for l in range(DEPTH):
        x = hybrid_layer(x, c, positions, rms_mix[l], rms_mlp[l], w_ada[l], b_ada[l], w_in[l],
                         q_norm_swa[l], k_norm_swa[l], q_norm_moba[l], k_norm_moba[l],
                         swa_sinks[l], w_o_swa[l], w_o_moba[l], w_out[l], w_up[l], w_down[l])
    return x
```

```python
import numpy as np
from contextlib import ExitStack
import concourse.bass as bass
import concourse.mybir as mybir
from concourse.bass_utils import run_bass_kernel_spmd

F32 = mybir.dt.float32
BF16 = mybir.dt.bfloat16
I32 = mybir.dt.int32
AF = mybir.ActivationFunctionType
ALU = mybir.AluOpType
AX = mybir.AxisListType

D = 1024
SEQ = 2048
NT = 16
NS = 4
DFF = 4096
INW = 4352
EPS = 1e-6
NEG = -1.0e5
SCALE = 0.125
DSIZE = {F32: 4, BF16: 2, I32: 4}


class Sched:
    def __init__(self):
        self.ops = []
        self.last_writer = {}
        self.readers = {}
        self.last_dma = {}
        self.marks = []

    def mark(self, label):
        self.marks.append((label, sum(1 for o in self.ops if o['eng'] == 'pe')))

    def add(self, eng, fn, reads=(), writes=(), dma_key=None):
        idx = len(self.ops)
        deps = set()
        for k in reads:
            w = self.last_writer.get(k)
            if w is not None:
                deps.add(w)
        raw = set(deps)
        for k in writes:
            w = self.last_writer.get(k)
            if w is not None:
                deps.add(w)
            for r in self.readers.get(k, ()):
                deps.add(r)
        for k in reads:
            self.readers.setdefault(k, []).append(idx)
        for k in writes:
            self.last_writer[k] = idx
            self.readers[k] = []
        if dma_key is not None:
            prev = self.last_dma.get(dma_key)
            if prev is not None:
                deps.add(prev)
            self.last_dma[dma_key] = idx
        deps.discard(idx)
        self.ops.append(dict(eng=eng, fn=fn, deps=deps, raw=raw, dma_key=dma_key, signal=False))
        return idx

    def emit(self, nc, stack, final_wait_keys=()):
        ops = self.ops
        fin = set()
        for k in final_wait_keys:
            w = self.last_writer.get(k)
            if w is not None:
                fin.add(w)
        if fin:
            ops.append(dict(eng='sp', fn=None, deps=fin, raw=set(), dma_key=None, signal=False))

        def needs_edge(op, d):
            p = ops[d]
            if p['dma_key'] is not None:
                return True
            if p['eng'] != op['eng']:
                return True
            return p['eng'] != 'pe'

        for op in ops:
            for d in op['deps']:
                p = ops[d]
                if p['dma_key'] is None and needs_edge(op, d):
                    p['signal'] = True
        engs = ['pe', 'act', 'dve', 'pool', 'sp']
        cnt = {e: 0 for e in engs}
        dcnt = {}
        for op in ops:
            if op['dma_key'] is not None:
                dcnt[op['dma_key']] = dcnt.get(op['dma_key'], 0) + 16
                op['dmaval'] = dcnt[op['dma_key']]
            elif op['signal']:
                cnt[op['eng']] += 1
                op['sigval'] = cnt[op['eng']]
        sems = {e: stack.enter_context(nc.semaphore('s_' + e)) for e in engs if cnt[e] > 0}
        dsems = {k: stack.enter_context(nc.semaphore('d_%s' % k)) for k in dcnt}
        per_eng = {e: [] for e in engs}
        for i, op in enumerate(ops):
            per_eng[op['eng']].append(i)
        block = stack.enter_context(nc.Block())

        def run(engname, eng):
            waited = {}
            for i in per_eng[engname]:
                op = ops[i]
                need = {}
                for d in op['deps']:
                    p = ops[d]
                    if p['dma_key'] is not None:
                        s, v = dsems[p['dma_key']], p['dmaval']
                    elif needs_edge(op, d):
                        s, v = sems[p['eng']], p['sigval']
                    else:
                        continue
                    kk = id(s)
                    if v > need.get(kk, (None, 0))[1]:
                        need[kk] = (s, v)
                for kk, (s, v) in need.items():
                    if waited.get(kk, 0) >= v:
                        continue
                    eng.wait_ge(s, v)
                    waited[kk] = v
                if op['fn'] is None:
                    continue
                ins = op['fn'](eng)
                if op['dma_key'] is not None:
                    ins.then_inc(dsems[op['dma_key']], 16)
                elif op['signal']:
                    ins.then_inc(sems[engname], 1)

        @block.tensor
        def _(e):
            run('pe', e)

        @block.scalar
        def _(e):
            run('act', e)

        @block.vector
        def _(e):
            run('dve', e)

        @block.gpsimd
        def _(e):
            run('pool', e)

        @block.sync
        def _(e):
            run('sp', e)


class Arena:
    def __init__(self, nc, st, name, nbytes, gran=512):
        self.name = name
        self.gran = gran
        self.nbytes = nbytes
        self.t = {BF16: st.enter_context(nc.sbuf_tensor(name, [128, nbytes // 2], BF16))}
        self.t[F32] = self.t[BF16].bitcast(F32)
        self.t[I32] = self.t[BF16].bitcast(I32)

    def view(self, off, shape, dt):
        return View(self, off, tuple(shape), dt)


class PsumBank:
    def __init__(self, nc, st, name):
        self.name = name
        self.gran = 2048
        self.t = {F32: st.enter_context(nc.psum_tensor(name, [128, 512], F32))}
        self.t[BF16] = self.t[F32].bitcast(BF16)

    def view(self, off, shape, dt):
        return View(self, off, tuple(shape), dt)


class View:
    def __init__(self, arena, off, shape, dt, p0=0, p1=128, idx=None, bc=None):
        self.arena, self.off, self.shape, self.dt = arena, off, shape, dt
        self.p0, self.p1 = p0, p1
        self.idx = idx if idx is not None else tuple(slice(0, n) for n in shape)
        self.bc = bc

    def __getitem__(self, key):
        if not isinstance(key, tuple):
            key = (key,)
        pk = key[0]
        p0, p1 = self.p0, self.p1
        if isinstance(pk, slice) and pk != slice(None):
            p0, p1 = self.p0 + (pk.start or 0), self.p0 + pk.stop
        rest = list(key[1:])
        new = []
        ri = 0
        for cur in self.idx:
            if isinstance(cur, int):
                new.append(cur)
                continue
            if ri < len(rest):
                r = rest[ri]
                ri += 1
                if isinstance(r, int):
                    new.append(cur.start + r)
                else:
                    a = cur.start + (r.start or 0)
                    b = cur.start + (r.stop if r.stop is not None else cur.stop - cur.start)
                    new.append(slice(a, b))
            else:
                new.append(cur)
        return View(self.arena, self.off, self.shape, self.dt, p0, p1, tuple(new))

    def part(self, p0, p1):
        return View(self.arena, self.off, self.shape, self.dt, self.p0 + p0, self.p0 + p1, self.idx)

    @property
    def ap(self):
        esz = DSIZE[self.dt]
        n = int(np.prod(self.shape))
        e0 = self.off // esz
        base = self.arena.t[self.dt][self.p0:self.p1, e0:e0 + n]
        if len(self.shape) > 1:
            names = ' '.join('d%d' % i for i in range(len(self.shape)))
            kw = {'d%d' % i: self.shape[i] for i in range(1, len(self.shape))}
            base = base.rearrange('p (%s) -> p %s' % (names, names), **kw)
        return base[(slice(None),) + tuple(self.idx)]

    @property
    def keys(self):
        esz = DSIZE[self.dt]
        strides = []
        s = esz
        for nn in reversed(self.shape):
            strides.append(s)
            s *= nn
        strides = strides[::-1]
        lo = self.off
        hi = self.off
        for i, cur in enumerate(self.idx):
            if isinstance(cur, int):
                lo += cur * strides[i]
                hi += cur * strides[i]
            else:
                lo += cur.start * strides[i]
                hi += (cur.stop - 1) * strides[i]
        hi += esz
        g = self.arena.gran
        halves = []
        if self.p0 < 64:
            halves.append(0)
        if self.p1 > 64:
            halves.append(1)
        return [(self.arena.name, gi, h) for gi in range(lo // g, (hi - 1) // g + 1) for h in halves]


class Raw:
    def __init__(self, ap, keys=()):
        self.ap = ap
        self.keys = list(keys)


def _host_consts():
    cb = np.zeros((128, 1664), np.float32)
    eye = np.eye(128, dtype=np.float32)
    cb[:, 0:128] = eye
    rt = np.zeros((128, 128), np.float32)
    for m in range(128):
        d = m % 64
        if d < 32:
            rt[m + 32, m] = -1.0
        else:
            rt[m - 32, m] = 1.0
    cb[:, 128:256] = rt
    bones = np.zeros((128, 128), np.float32)
    bones[0:64, 0:64] = 1.0 / 64
    bones[64:128, 64:128] = 1.0 / 64
    cb[:, 256:384] = bones
    cb[:, 384:512] = 1.0 / 1024
    kk = np.arange(128)[:, None]
    qq = np.arange(128)[None, :]
    for r4 in range(4):
        cb[:, 512 + r4 * 128:512 + (r4 + 1) * 128] = np.where(kk <= qq, 0.0, NEG)
        cb[:, 1024 + r4 * 128:1024 + (r4 + 1) * 128] = np.where(kk > qq, 0.0, NEG)
    cb[:, 1536 + 64:1536 + 128] = 1.0
    cf = np.zeros((128, 128 + 8), np.float32)
    cf[:, 0:128] = eye
    p = np.arange(128)
    cf[:, 128] = (10000.0 ** (-((p % 32).astype(np.float32)) / 32.0)).astype(np.float32)
    cf[:, 129] = -np.pi
    cf[:, 130] = EPS
    return cb, cf


def build_program(layers, n_total_layers=2, dbg=None, stop=None):
    nc = bass.Bass("TRN2", target_bir_lowering=False)
    L = n_total_layers
    x_d = nc.dram_tensor("x", [SEQ, D], F32, kind="ExternalInput").ap()
    out_d = nc.dram_tensor("out", [SEQ, D], F32, kind="ExternalOutput").ap()
    pos_d = nc.dram_tensor("pos", [SEQ], I32, kind="ExternalInput").ap()
    cT_d = nc.dram_tensor("cT", [128, 8], F32, kind="ExternalInput").ap()
    vec_d = nc.dram_tensor("vecs", [128, L * 68], F32, kind="ExternalInput").ap()
    sink_d = nc.dram_tensor("sinks", [L, 8], F32, kind="ExternalInput").ap()
    cb_d = nc.dram_tensor("cb", [128, 1664], F32, kind="ExternalInput").ap()
    cf_d = nc.dram_tensor("cf", [128, 136], F32, kind="ExternalInput").ap()
    w_ada_d = nc.dram_tensor("w_ada", [L, D, 6 * D], F32, kind="ExternalInput").ap()
    w_in_d = nc.dram_tensor("w_in", [L, D, INW], F32, kind="ExternalInput").ap()
    w_oa_d = nc.dram_tensor("w_o_swa", [L, 512, D], F32, kind="ExternalInput").ap()
    w_ob_d = nc.dram_tensor("w_o_moba", [L, 512, D], F32, kind="ExternalInput").ap()
    w_out_d = nc.dram_tensor("w_out", [L, D, D], F32, kind="ExternalInput").ap()
    w_up_d = nc.dram_tensor("w_up", [L, D, DFF], F32, kind="ExternalInput").ap()
    w_dn_d = nc.dram_tensor("w_down", [L, DFF, D], F32, kind="ExternalInput").ap()

    S = Sched()
    dbg_out = {}
    with ExitStack() as st:
        XA = Arena(nc, st, "XA", 65536, gran=2048)
        AA = Arena(nc, st, "AA", 32768, gran=1024)
        BIG = Arena(nc, st, "BIG", 65536, gran=256)
        TAB = Arena(nc, st, "TAB", 12288, gran=1024)
        SB = Arena(nc, st, "SB", 16384, gran=256)
        SF = Arena(nc, st, "SF", 6144, gran=2048)
        SF2 = Arena(nc, st, "SF2", 4096, gran=1024)
        CB = Arena(nc, st, "CB", 1664 * 2, gran=256)
        CF = Arena(nc, st, "CF", 136 * 4 + 32, gran=4096)
        VEC = Arena(nc, st, "VEC", 4096, gran=64)
        PS = [PsumBank(nc, st, "ps%d" % i) for i in range(8)]

        xT = XA.view(0, (8, SEQ), F32)
        AT = AA.view(0, (8, SEQ), BF16)
        ang = TAB.view(0, (SEQ,), F32)
        cos_s = TAB.view(8192, (512,), F32)
        sin_s = TAB.view(10240, (512,), F32)
        hTs = SB.view(0, (8, 512), BF16)
        sqb = SB.view(8192, (512,), BF16)
        xnb = SB.view(9216, (512,), BF16)
        Pb = [SB.view(10240, (512,), BF16), SB.view(11264, (512,), BF16)]
        biasT = SB.view(12288, (512,), BF16)
        biasq = SB.view(13312, (64,), BF16)
        mixs = SB.view(8192, (8, 512), BF16)
        ubuf = [SB.view(0, (8, 512), BF16), SB.view(8192, (8, 512), BF16)]
        rstd = SF.view(0, (512,), F32)
        t1 = SF.view(2048, (512,), F32)
        t2 = SF.view(4096, (512,), F32)
        identb = CB.view(0, (128,), BF16)
        Rt = CB.view(256, (128,), BF16)
        Bones = CB.view(512, (128,), BF16)
        Oones = CB.view(768, (128,), BF16)
        triL4 = CB.view(1024, (512,), BF16)
        triU4 = CB.view(2048, (512,), BF16)
        triL = CB.view(1024, (128,), BF16)
        onespad = CB.view(3072, (128,), BF16)
        identf = CF.view(0, (128,), F32)
        invf = CF.view(512, (1,), F32)
        negpi = CF.view(516, (1,), F32)
        epsc = CF.view(520, (1,), F32)
        def vec(l, off, n, dt=F32):
            return VEC.view(l * 1536 + off, (n,), dt)
        silc = VEC.view(3072, (8,), BF16)
        cin = VEC.view(3104, (8,), F32)
        esink = VEC.view(3136, (16,), F32)
        kmean = VEC.view(3328, (4, 8), F32)
        kmpad = VEC.view(3456, (4, 64), BF16)
        gsb = VEC.view(3968, (8,), F32)

        psc = [0]

        def mm(out, lhsT, rhs, start, stop):
            r = list(lhsT.keys) + list(rhs.keys)
            if not start:
                r += list(out.keys)
            S.add('pe', lambda e, o=out.ap, l=lhsT.ap, rr=rhs.ap: e.matmul(o, l, rr, start=start, stop=stop),
                  reads=r, writes=out.keys)

        def tr(out, in_, ident):
            S.add('pe', lambda e, o=out.ap, i=in_.ap, d=ident.ap: e.transpose(o, i, d),
                  reads=list(in_.keys) + list(ident.keys), writes=out.keys)

        def act(out, in_, func, bias=None, scale=1.0, accum=None):
            r = list(in_.keys)
            kw = {}
            if bias is not None:
                if isinstance(bias, (View, Raw)):
                    r += list(bias.keys)
                    kw['bias'] = bias.ap
                else:
                    kw['bias'] = bias
            if isinstance(scale, (View, Raw)):
                r += list(scale.keys)
                kw['scale'] = scale.ap
            else:
                kw['scale'] = scale
            w = list(out.keys)
            if accum is not None:
                kw['accum_out'] = accum.ap
                w += list(accum.keys)
            S.add('act', lambda e, o=out.ap, i=in_.ap: e.activation(out=o, in_=i, func=func, **kw),
                  reads=r, writes=w)

        def tt(out, in0, in1, op, eng='dve'):
            S.add(eng, lambda e, o=out.ap, a=in0.ap, b=in1.ap: e.tensor_tensor(out=o, in0=a, in1=b, op=op),
                  reads=list(in0.keys) + list(in1.keys), writes=out.keys)

        def ts(out, in0, s1, s2, op0, op1=None, eng='dve'):
            r = list(in0.keys)
            a1 = s1
            a2 = s2
            if isinstance(s1, (View, Raw)):
                r += list(s1.keys)
                a1 = s1.ap
            if isinstance(s2, (View, Raw)):
                r += list(s2.keys)
                a2 = s2.ap
            if op1 is None:
                S.add(eng, lambda e, o=out.ap, a=in0.ap: e.tensor_scalar(out=o, in0=a, scalar1=a1, scalar2=None, op0=op0),
                      reads=r, writes=out.keys)
            else:
                S.add(eng, lambda e, o=out.ap, a=in0.ap: e.tensor_scalar(out=o, in0=a, scalar1=a1, scalar2=a2, op0=op0, op1=op1),
                      reads=r, writes=out.keys)

        def stt(out, in0, sc, in1, op0, op1, eng='dve'):
            r = list(in0.keys) + list(in1.keys)
            a = sc
            if isinstance(sc, (View, Raw)):
                r += list(sc.keys)
                a = sc.ap
            S.add(eng, lambda e, o=out.ap, x=in0.ap, y=in1.ap: e.scalar_tensor_tensor(out=o, in0=x, scalar=a, in1=y, op0=op0, op1=op1),
                  reads=r, writes=out.keys)

        def cp(out, in_, eng='dve'):
            if eng == 'act':
                S.add('act', lambda e, o=out.ap, i=in_.ap: e.copy(out=o, in_=i), reads=in_.keys, writes=out.keys)
            else:
                S.add(eng, lambda e, o=out.ap, i=in_.ap: e.tensor_copy(out=o, in_=i), reads=in_.keys, writes=out.keys)

        def recip(out, in_):
            S.add('dve', lambda e, o=out.ap, i=in_.ap: e.reciprocal(out=o, in_=i), reads=in_.keys, writes=out.keys)

        def memset(out, val, eng='dve'):
            S.add(eng, lambda e, o=out.ap: e.memset(o, val), writes=out.keys)

        dmac = [0]

        def dma(out, in_, eng='sp', key=None):
            if key is None:
                dmac[0] += 1
                key = '%s%d' % ('g' if eng == 'pool' else 'q', dmac[0] % 20)
            S.add(eng, lambda e, o=out.ap, i=in_.ap: e.dma_start(out=o, in_=i),
                  reads=in_.keys, writes=out.keys, dma_key=key)

        def dump(name, v, shape, dt):
            if dbg is None or name not in dbg:
                return
            d = nc.dram_tensor("dbg_" + name, list(shape), dt, kind="ExternalOutput").ap()
            dma(Raw(d, ['dbg_' + name]), v, 'sp', key='dbg')
            dbg_out[name] = 1

        gen_banks = [[0, 1, 2, 3, 4, 5, 6, 7]]

        def psb():
            lst = gen_banks[0]
            b = PS[lst[psc[0] % len(lst)]]
            psc[0] += 1
            return b

        dma(CB.view(0, (1664,), BF16), Raw(cb_d), 'pool')
        dma(CF.view(0, (136,), F32), Raw(cf_d), 'sp')
        for l in range(L):
            dma(vec(l, 0, 68), Raw(vec_d[:, l * 68:(l + 1) * 68]), 'sp')
        dma(cin, Raw(cT_d), 'sp')
        dma(esink, Raw(sink_d.rearrange("l h -> (l h)").partition_broadcast(128)), 'sp')
        act(silc, cin, AF.Silu)
        act(esink, esink, AF.Exp)

        xin = BIG.view(0, (4, D), F32)
        for t in range(NT):
            xi = xin[:, t % 4]
            dma(xi, Raw(x_d[t * 128:(t + 1) * 128, :]), 'sp', key='xin%d' % (t % 4))
            for hb in range(2):
                bank = psb()
                pv = bank.view(0, (4, 128), F32)
                for kk in range(4):
                    k = hb * 4 + kk
                    tr(pv[:, kk], xi[:, k * 128:(k + 1) * 128], identf)
                cp(xT[:, hb * 4:hb * 4 + 4, t * 128:(t + 1) * 128], pv, eng='act' if hb else 'dve')

        posi = BIG.view(16384, (SEQ,), I32)
        posf = BIG.view(24576, (SEQ,), F32)
        dma(posi, Raw(pos_d.partition_broadcast(128)), 'sp')
        cp(posf, posi)
        ts(ang, posf, invf, None, ALU.mult)

        C1 = 6.28125
        C2 = float(2 * np.pi - 6.28125)
        TWO_PI = float(2 * np.pi)

        def rope_tables(s, cs=None, C=None, Ci=None):
            sl = slice(s * 512, (s + 1) * 512)
            x = ang[:, sl]
            B_, A_ = cs if cs is not None else (cos_s, sin_s)
            C = t1 if C is None else C
            Ci = SF.view(4096, (512,), I32) if Ci is None else Ci
            ts(C, x, 1.0 / TWO_PI, None, ALU.mult)
            cp(Ci, C)
            cp(C, Ci)
            stt(A_, C, -C1, x, ALU.mult, ALU.add)
            stt(A_, C, -C2, A_, ALU.mult, ALU.add)
            ts(C, A_, float(np.pi), None, ALU.is_gt)
            stt(A_, C, -TWO_PI, A_, ALU.mult, ALU.add)
            ts(B_, A_, float(np.pi / 2), None, ALU.add)
            ts(C, B_, float(np.pi), None, ALU.is_gt)
            stt(B_, C, -TWO_PI, B_, ALU.mult, ALU.add)
            act(A_, A_, AF.Sin)
            act(B_, B_, AF.Sin)

        def mod_groups(l, part, bufs, bank=None):
            c0, c1 = (0, 16) if part == 0 else (16, 48)
            ngrp = (c1 - c0) // 2
            stt_ = {}
            fns = []
            for gi in range(ngrp):
                def fn(gi=gi):
                    if gi == 0:
                        stt_['pm'] = (psb() if bank is None else bank).view(0, (48,), F32)
                    pm = stt_['pm']
                    col0 = c0 + gi * 2
                    wA = bufs[gi % len(bufs)]
                    dma(wA, Raw(w_ada_d[l, :, col0 * 128:(col0 + 2) * 128].rearrange("(kc p) c -> p kc c", p=128)),
                        'pool', key='wA%d_%d' % (part, gi % len(bufs)))
                    for j in range(2):
                        for k in range(8):
                            mm(pm[:, col0 + j:col0 + j + 1], wA[:, k, j * 128:(j + 1) * 128], silc[:, k:k + 1],
                               start=(k == 0), stop=(k == 7))
                    if gi == ngrp - 1:
                        modv = vec(l, 272, 48)
                        tt(modv[:, c0:c1], pm[:, c0:c1], vec(l, 0, 68)[:, 16 + c0:16 + c1], ALU.add)
                        if part == 0:
                            stt(vec(l, 464, 8), modv[:, 8:16], 1.0, vec(l, 0, 68)[:, 0:8], ALU.add, ALU.mult)
                        else:
                            stt(vec(l, 496, 8), modv[:, 32:40], 1.0, vec(l, 0, 68)[:, 8:16], ALU.add, ALU.mult)
                fns.append(fn)
            return fns

        def A1(l):
            return vec(l, 464, 8)

        def A2(l):
            return vec(l, 496, 8)

        def MODV(l):
            return vec(l, 272, 48)

        def GAIN(l, i):
            return vec(l, 0, 68)[:, 64 + i:65 + i]

        def norm_tile(s, acol, bcol, dest):
            sl = slice(s * 512, (s + 1) * 512)
            bank = psb()
            pm = bank.view(0, (512,), F32)
            for k in range(8):
                act(dest[:, k], xT[:, k, sl], AF.Square)
                mm(pm, Oones, dest[:, k], start=(k == 0), stop=(k == 7))
            act(rstd, pm, AF.Ln, bias=epsc)
            act(rstd, rstd, AF.Exp, scale=-0.5)
            for k in range(8):
                stt(t1 if k % 2 == 0 else t2, xT[:, k, sl], acol[:, k:k + 1], rstd, ALU.mult, ALU.mult)
                act(dest[:, k], t1 if k % 2 == 0 else t2, AF.Identity, bias=bcol[:, k:k + 1])

        qksets = [(sqb, xnb, rstd, t1),
                  (SF2.view(2048, (512,), BF16), SF2.view(3072, (512,), BF16), t2, SF2.view(0, (512,), F32))]
        qkc = [0]
        pinc = [0]
        qkn = [2]
        pmbanks = [[3]]

        def qk_item(proj, gain, dest, cs=None):
            st_ = {}

            def s0():
                st_['pin'] = PS[pinc[0] % 3].view(0, (512,), F32)
                pinc[0] += 1
                st_['set'] = qksets[qkc[0] % qkn[0]]
                qkc[0] += 1
                proj(st_['pin'])

            def s1():
                sq_, xn_, r_, tt1_ = st_['set']
                pin = st_['pin']
                act(sq_, pin, AF.Square)
                pm = PS[pmbanks[0][qkc[0] % len(pmbanks[0])]].view(0, (512,), F32)
                mm(pm, Bones, sq_, True, True)
                act(r_, pm, AF.Ln, bias=epsc)
                act(r_, r_, AF.Exp, scale=-0.5)
                stt(xn_, pin, gain, r_, ALU.mult, ALU.mult)

            def s2():
                sq_, xn_, r_, tt1_ = st_['set']
                pr = PS[4 + st_.setdefault('prb', qkc[0] % 2)].view(0, (512,), F32)
                mm(pr, Rt, xn_, True, True)
                cos_, sin_ = cs if cs is not None else (cos_s, sin_s)
                tt(tt1_, xn_, cos_, ALU.mult)
                tt(r_, pr, sin_, ALU.mult)
                tt(dest, tt1_, r_, ALU.add)
            return (s0, s1, s2)

        def run_items(items, fillers=None, after_proj=None, mid=None):
            n = len(items)
            for i in range(n + 2):
                if i == 3 and mid is not None:
                    mid()
                if fillers:
                    fillers.pop(0)()
                if i < n:
                    items[i][0]()
                if i == n - 1 and after_proj is not None:
                    after_proj()
                if 0 <= i - 1 < n and items[i - 1][1] is not None:
                    items[i - 1][1]()
                if 0 <= i - 2 < n and items[i - 2][2] is not None:
                    items[i - 2][2]()

        BIGROW = 32768
        AAROW = 16384
        CBROW = 896
        VECROW = 2048
        rot = {'sc': 0, 'po': 0, 'pb': 0, 'sq': 0, 'va': 0, 'nz': 0}

        def wdma(dst, src2d):
            dma(dst, Raw(src2d.rearrange("(kc p) c -> p kc c", p=128)), 'pool')

        def layer(l):
            S.mark('L%d mod' % l)
            if l == layers[0]:
                for fn in mod_groups(l, 0, [BIG.view(32768, (8, 256), BF16), BIG.view(36864, (8, 256), BF16)]):
                    fn()
                modfill = mod_groups(l, 1, [AA.view(24576, (8, 256), BF16), AA.view(28672, (8, 256), BF16)], PS[7])
            else:
                modfill = []
            if stop == 'mod':
                return
            modv = MODV(l)
            a1, a2 = A1(l), A2(l)
            sh1, gt1, sh2, gt2 = modv[:, 0:8], modv[:, 16:24], modv[:, 24:32], modv[:, 40:48]
            kaT = BIG.view(0, (2, SEQ), BF16)
            kbT = BIG.view(8192, (4, SEQ), BF16)
            kbT4 = BIG.view(8192, (4, 8, 256), BF16)
            VOFF = 24576
            vst = BIG.view(VOFF, (16, 640), BF16)
            wqa = BIG.view(45312, (8, 512), BF16)
            wqb = BIG.view(45312 + 8192, (8, 512), BF16)
            wkva = AA.view(0, (8, 256), BF16)
            wkb = AA.view(4096, (8, 512), BF16)
            wvb = AA.view(12288, (8, 512), BF16)
            wkad = AA.view(20480, (8, 2, 2, 64), BF16)
            wdma(wkva, w_in_d[l, :, 512:768])
            for g in range(2):
                for dd in range(2):
                    wdma(wkad[:, :, g, dd], w_in_d[l, :, 512 + g * 64:512 + (g + 1) * 64])
            wdma(wkb, w_in_d[l, :, 1280:1792])
            wdma(wvb, w_in_d[l, :, 1792:2304])
            wdma(wqa, w_in_d[l, :, 0:512])
            csA = [(cos_s, sin_s), (BIG.view(53504, (512,), F32), BIG.view(55552, (512,), F32))]
            ropeC, ropeCi = BIG.view(57600, (512,), F32), BIG.view(59648, (512,), I32)

            vflat = BIG.view(VOFF, (16 * 640 + 64,), BF16)
            memset(vflat[:, 16 * 640:16 * 640 + 64], 0.0)

            def vprep(c, col):
                va = vaug[rot['va'] % 4]
                rot['va'] += 1
                cp(va[:, 0:64], vst[:, c, col:col + 64], eng='pool')
                return va

            S.mark('L%d A' % l)
            norm_tile(0, a1, sh1, hTs)
            rope_tables(0, csA[0], ropeC, ropeCi)
            for s in range(NS):
                sl = slice(s * 512, (s + 1) * 512)
                gen_banks[0] = [6] if modfill else [6, 7]
                items = []
                wkad2 = AA.view(20480, (8, 2, 128), BF16)

                def mk_ka(g):
                    def proj(pin):
                        for k in range(8):
                            mm(pin, wkad2[:, k, g], hTs[:, k], k == 0, k == 7)
                    return qk_item(proj, GAIN(l, 1), kaT[:, g, sl], csA[s % 2])

                def mk_kb(p):
                    def proj(pin):
                        for k in range(8):
                            mm(pin, wkb[:, k, p * 128:(p + 1) * 128], hTs[:, k], k == 0, k == 7)
                    return qk_item(proj, GAIN(l, 3), kbT[:, p, sl], csA[s % 2])

                def mk_v(tq):
                    def s0():
                        c = s * 4 + tq
                        tsl = slice(tq * 128, (tq + 1) * 128)
                        pva = psb().view(0, (128,), F32)
                        for k in range(8):
                            mm(pva, hTs[:, k, tsl], wkva[:, k, 128:256], k == 0, k == 7)
                        cp(vst[:, c, 0:128], pva, eng='act')
                        pvb = psb().view(0, (512,), F32)
                        for k in range(8):
                            mm(pvb, hTs[:, k, tsl], wvb[:, k], k == 0, k == 7)
                        cp(vst[:, c, 128:640], pvb, eng='dve')
                    return (s0, None, None)
                kitems = [mk_ka(0), mk_ka(1)] + [mk_kb(p) for p in range(4)]
                vitems = [mk_v(tq) for tq in range(4)]
                items = [vitems[0], vitems[1], kitems[0], vitems[2], kitems[1], vitems[3]] + kitems[2:]
                run_items(items, modfill,
                          after_proj=(lambda s=s: norm_tile(s + 1, a1, sh1, hTs)) if s + 1 < NS else None,
                          mid=(lambda s=s: rope_tables(s + 1, csA[(s + 1) % 2], ropeC, ropeCi)) if s + 1 < NS else None)
            while modfill:
                modfill.pop(0)()
            gen_banks[0] = [0, 1, 2, 3, 4, 5, 6, 7]
            if stop == 'A':
                return
            dump('kaT', kaT, (128, 2, SEQ), BF16)
            dump('kbT', kbT, (128, 4, SEQ), BF16)
            dump('vst', vst, (128, 16, 640), BF16)

            qkn[0] = 2
            wdma(wqb, w_in_d[l, :, 768:1280])
            S.add('dve', lambda e: e.tensor_reduce(out=kmean.ap, in_=kbT4.ap, axis=AX.X, op=ALU.add),
                  reads=kbT4.keys, writes=kmean.keys)
            memset(kmpad, 0.0)
            for p in range(4):
                for hf in range(2):
                    h = 2 * p + hf
                    ts(kmpad[hf * 64:(hf + 1) * 64, p, h * 8:(h + 1) * 8], kmean[hf * 64:(hf + 1) * 64, p],
                       1.0 / 256, None, ALU.mult)

            biasq128 = SB.view(13312, (128,), BF16)
            memset(biasq128, 0.0)
            memset(biasT, 0.0)
            qz = [[BIG.view(61696, (512,), BF16), BIG.view(62720, (512,), BF16)],
                  [SB.view(13568, (512,), BF16), SB.view(14592, (512,), BF16)]]
            for a_ in range(2):
                for b_ in range(2):
                    memset(qz[a_][b_], 0.0)
            qrot = [0, 0]
            gs3 = VEC.view(1024, (8, 8), F32)
            biasq3 = SB.view(13312, (8, 8), BF16)
            E_GS = 1024 // 4
            VECROWF = 1024
            Pb3 = [Pb[0], Pb[1], BIG.view(63744, (512,), BF16), SF2.view(0, (512,), BF16)]
            vaug = [BIG.view(64768 + 256 * i_, (128,), BF16) for i_ in range(3)] + [SB.view(15616, (128,), BF16)]
            for i_ in range(4):
                memset(vaug[i_][:, 64:128], 1.0)
            norm_tile(0, a1, sh1, hTs)
            rope_tables(0)
            for s in range(NS):
                gen_banks[0] = [6]
                pmbanks[0] = [3, 7]
                S.mark('L%d B%d qproj' % (l, s))
                sl = slice(s * 512, (s + 1) * 512)
                def mk_q(p):
                    w = wqa if p < 4 else wqb
                    pc = p % 4

                    def proj(pin):
                        for k in range(8):
                            mm(pin, w[:, k, pc * 128:(pc + 1) * 128], hTs[:, k], k == 0, k == 7)
                    return qk_item(proj, GAIN(l, 0 if p < 4 else 2), AT[:, p, sl])
                run_items([mk_q(p) for p in range(8)],
                          after_proj=(lambda s=s: norm_tile(s + 1, a1, sh1, hTs)) if s + 1 < NS else None)
                if s + 1 < NS:
                    rope_tables(s + 1)
                if s == 0:
                    dump('qT0', AT[:, :, 0:512], (128, 8, 512), BF16)
                if stop == 'Bq':
                    return
                gen_banks[0] = [0]
                pmbanks[0] = [3]
                S.mark('L%d B%d bias' % (l, s))
                for tq in range(4):
                    i = s * 4 + tq
                    qb = i // 2
                    qsl = slice(s * 512 + tq * 128, s * 512 + (tq + 1) * 128)
                    memset(biasq3, NEG)
                    if qb < 4:
                        memset(biasq3[:, :, 0:qb + 1], 0.0)
                    else:
                        nb = qb
                        pgb = psb()
                        pg = pgb.view(0, (64,), F32)
                        for p4 in range(4):
                            mm(pg, AT[:, 4 + p4, qsl], kmpad[:, p4], p4 == 0, p4 == 3)
                        cp(gs3, pgb.view(0, (8, 8), F32))
                        cmpv = SF.view(2048, (8, nb, nb), F32)
                        rkv = SF.view(4096, (8, nb), F32)
                        gb1 = Raw(bass.AP(VEC.t[F32], E_GS, [[VECROWF, 128], [8, 8], [0, nb], [1, nb]]), gs3.keys)
                        gb0 = Raw(bass.AP(VEC.t[F32], E_GS, [[VECROWF, 128], [8, 8], [1, nb], [0, nb]]), gs3.keys)
                        tt(cmpv, gb1, gb0, ALU.is_gt)
                        S.add('dve', lambda e, o=rkv.ap, i_=cmpv.ap: e.tensor_reduce(out=o, in_=i_, axis=AX.X, op=ALU.add),
                              reads=cmpv.keys, writes=rkv.keys)
                        ts(rkv, rkv, -2.0, 0.0, ALU.add, ALU.max)
                        ts(biasq3[:, :, 0:nb], rkv, NEG, None, ALU.mult)
                        memset(biasq3[:, :, nb:nb + 1], 0.0)
                        if i == 8:
                            dump('gs8', gs3, (128, 8, 8), F32)
                            dump('bq8', biasq3, (128, 8, 8), BF16)
                            dump('rk8', rkv, (128, 8, nb), F32)
                    pt = psb().view(0, (128,), BF16)
                    tr(pt, biasq128, identb)
                    cp(biasT[0:64, tq * 128:(tq + 1) * 128], pt.part(0, 64))
                if stop == 'Bb':
                    return
                S.mark('L%d B%d attn' % (l, s))
                units = []

                def swa_units(g, tq):
                    i = s * 4 + tq
                    qsl = slice(s * 512 + tq * 128, s * 512 + (tq + 1) * 128)
                    chunks = [i - 1, i] if i > 0 else [i]
                    po = PS[5 + rot['po'] % 3].view(0, (512,), F32)
                    pz = po
                    rot['po'] += 1
                    qzb = [qz[0][qrot[0] % 2], qz[1][qrot[1] % 2]]
                    qrot[0] += 1
                    qrot[1] += 1
                    for ci, c in enumerate(chunks):
                        sc = PS[1 + rot['sc'] % 4].view(0, (512,), F32)
                        rot['sc'] += 1
                        pb = Pb3[rot['pb'] % 4]
                        rot['pb'] += 1
                        triap = triL4 if c == i else triU4

                        vh = {}

                        def score(sc=sc, c=c, triap=triap, qzb=qzb, ci=ci, vh=vh):
                            vh['va'] = vprep(c, g * 64)
                            if ci == 0:
                                for j in range(4):
                                    h = 4 * g + j
                                    hb = (h % 2) * 64
                                    cp(qzb[h % 2][hb:hb + 64, j * 128:(j + 1) * 128], AT[hb:hb + 64, h // 2, qsl])
                            mm(sc, identb, triap, True, False)
                            for j in range(4):
                                h = 4 * g + j
                                mm(sc[:, j * 128:(j + 1) * 128], kaT[:, g, c * 128:(c + 1) * 128],
                                   qzb[h % 2][:, j * 128:(j + 1) * 128], False, j == 3)

                        def ex(sc=sc, pb=pb):
                            act(pb, sc, AF.Exp, scale=SCALE)

                        last = (ci == len(chunks) - 1)

                        def pv(po=po, pb=pb, ci=ci, last=last, vh=vh):
                            mm(po, vh['va'], pb, ci == 0, last)

                        def post(pz=po, po=po, last=last):
                            if last:
                                rstd = [SF.view(0, (512,), F32), SF.view(2048, (512,), F32), SF.view(4096, (512,), F32)][rot['nz'] % 3]
                                rot['nz'] += 1
                                for j in range(4):
                                    h = 4 * g + j
                                    ts(rstd[64:128, j * 128:(j + 1) * 128], pz[64:128, j * 128:(j + 1) * 128],
                                       esink[64:128, l * 8 + h:l * 8 + h + 1], None, ALU.add)
                                act(rstd.part(64, 128), rstd.part(64, 128), AF.Ln)
                                act(rstd.part(64, 128), rstd.part(64, 128), AF.Exp, scale=-1.0)
                                for j in range(4):
                                    h = 4 * g + j
                                    hb = (h % 2) * 64
                                    tt(AT[hb:hb + 64, h // 2, qsl], po[0:64, j * 128:(j + 1) * 128],
                                       rstd[64:128, j * 128:(j + 1) * 128], ALU.mult)
                        units.append((score, ex, pv, post))

                def moba_units(h):
                    hb = (h % 2) * 64
                    pm_ = h // 2
                    pr_ = 4 + pm_
                    po = PS[5 + rot['po'] % 3].view(0, (512,), F32)
                    pz = po
                    rot['po'] += 1
                    qzh = qz[h % 2][qrot[h % 2] % 2]
                    qrot[h % 2] += 1
                    nch = 4 * s + 4
                    for j in range(nch):
                        cl = j - 4 * s
                        q0 = 0 if cl < 0 else cl * 128
                        b = j // 2
                        sc = PS[1 + rot['sc'] % 4].view(0, (512,), F32)
                        rot['sc'] += 1
                        pb = Pb3[rot['pb'] % 4]
                        rot['pb'] += 1
                        r = h * 8 + b
                        sel = Raw(identb[:, r:r + 1].ap.to_broadcast([128, 128]), identb.keys)

                        vh = {}

                        def score(sc=sc, j=j, cl=cl, q0=q0, sel=sel, vh=vh):
                            vh['va'] = vprep(j, 128 + h * 64)
                            if j == 0:
                                cp(qzh[hb:hb + 64], AT[hb:hb + 64, pr_, sl])
                            need_bias = (s >= 2) and (cl < 2)
                            if need_bias:
                                mm(sc[:, q0:512], sel, biasT[:, q0:512], True, False)
                            mm(sc[:, q0:512], kbT[:, pm_, j * 128:(j + 1) * 128], qzh[:, q0:512], not need_bias, cl < 0)
                            if cl >= 0:
                                mm(sc[:, q0:q0 + 128], identb, triL, False, True)

                        def ex(sc=sc, pb=pb, q0=q0):
                            act(pb[:, q0:512], sc[:, q0:512], AF.Exp, scale=SCALE)

                        def pv(po=po, pb=pb, j=j, q0=q0, vh=vh):
                            mm(po[:, q0:512], vh['va'], pb[:, q0:512], j == 0, j == nch - 1)

                        def post(pz=po, po=po, j=j):
                            if j == nch - 1:
                                rstd = [SF.view(0, (512,), F32), SF.view(2048, (512,), F32), SF.view(4096, (512,), F32)][rot['nz'] % 3]
                                rot['nz'] += 1
                                act(rstd.part(64, 128), pz.part(64, 128), AF.Ln)
                                act(rstd.part(64, 128), rstd.part(64, 128), AF.Exp, scale=-1.0)
                                tt(AT[hb:hb + 64, pr_, sl], po[0:64], rstd[64:128], ALU.mult)
                        units.append((score, ex, pv, post))

                for g in range(2):
                    for tq in range(4):
                        swa_units(g, tq)
                if stop not in ('Bs', 'Bs1', 'Bs2', 'Bs0', 'BsX'):
                    for h in range(8):
                        moba_units(h)
                SK = 3
                for ui in range(min(SK, len(units))):
                    units[ui][0]()
                for ui in range(len(units) + 2):
                    if ui + SK < len(units):
                        units[ui + SK][0]()
                    if ui < len(units):
                        units[ui][1]()
                        units[ui][2]()
                    if 0 <= ui - 2 < len(units):
                        units[ui - 2][3]()
                if stop in ('Bs', 'Bm', 'Bs1', 'Bs2', 'Bs0', 'BsX'):
                    return
            gen_banks[0] = [0, 1, 2, 3, 4, 5, 6, 7]
            if stop == 'B':
                return
            dump('AT', AT, (128, 8, SEQ), BF16)

            S.mark('L%d C' % l)
            wg = [BIG.view(q * 8192, (8, 512), BF16) for q in range(4)]
            woa = [BIG.view(32768 + q * 4096, (4, 512), BF16) for q in range(2)]
            wob = [BIG.view(40960 + q * 4096, (4, 512), BF16) for q in range(2)]
            wout = [BIG.view(49152 + q * 8192, (8, 512), BF16) for q in range(2)]
            for q in range(2):
                wdma(wg[q], w_in_d[l, :, 2304 + q * 512:2304 + (q + 1) * 512])
                wdma(wg[2 + q], w_in_d[l, :, 2304 + (2 + q) * 512:2304 + (3 + q) * 512])
                wdma(woa[q], w_oa_d[l, :, q * 512:(q + 1) * 512])
                wdma(wob[q], w_ob_d[l, :, q * 512:(q + 1) * 512])
            for q in range(2):
                wdma(wout[q], w_out_d[l, :, q * 512:(q + 1) * 512])
            norm_tile(0, a1, sh1, hTs)
            for s in range(NS):
                sl = slice(s * 512, (s + 1) * 512)
                for j in range(8):
                    qd, jc = j // 4, (j % 4) * 128
                    p1 = psb().view(0, (512,), F32)
                    for k in range(8):
                        mm(p1, wg[qd][:, k, jc:jc + 128], hTs[:, k], k == 0, k == 7)
                    p2 = psb().view(0, (512,), F32)
                    for k in range(8):
                        mm(p2, wg[2 + qd][:, k, jc:jc + 128], hTs[:, k], k == 0, k == 7)
                    p3 = psb().view(0, (512,), F32)
                    for k in range(4):
                        mm(p3, woa[qd][:, k, jc:jc + 128], AT[:, k, sl], k == 0, k == 3)
                    p4 = psb().view(0, (512,), F32)
                    for k in range(4):
                        mm(p4, wob[qd][:, k, jc:jc + 128], AT[:, 4 + k, sl], k == 0, k == 3)
                    act(rstd, p1, AF.Sigmoid)
                    act(t2, p2, AF.Sigmoid)
                    tt(t1, rstd, p3, ALU.mult)
                    tt(t2, t2, p4, ALU.mult)
                    tt(mixs[:, j], t1, t2, ALU.add)
                if s + 1 < NS:
                    norm_tile(s + 1, a1, sh1, hTs)
                for j in range(8):
                    qd, jc = j // 4, (j % 4) * 128
                    p5 = psb().view(0, (512,), F32)
                    for k in range(8):
                        mm(p5, wout[qd][:, k, jc:jc + 128], mixs[:, k], k == 0, k == 7)
                    stt(xT[:, j, sl], p5, gt1[:, j:j + 1], xT[:, j, sl], ALU.mult, ALU.add)
            if stop == 'C':
                return
            dump('xmid', xT, (128, 8, SEQ), F32)

            S.mark('L%d MLPnorm' % l)
            hT2 = AA.view(0, (8, SEQ), BF16)
            for s in range(NS):
                norm_tile(s, a2, sh2, hT2[:, :, s * 512:(s + 1) * 512])
            sqs = [rstd, t1, t2]
            nxt = []
            if l + 1 < L and (l + 1) in layers:
                mb = [TAB.view(8192, (8, 256), BF16), SF2.view(0, (8, 256), BF16)]
                nxt = mod_groups(l + 1, 0, mb, PS[7]) + mod_groups(l + 1, 1, mb, PS[7])
                gen_banks[0] = [0, 1, 2, 3, 4, 5, 6]
            for g in range(4):
                S.mark('L%d MLP g%d' % (l, g))
                bo = (g % 2) * 32768
                wup = [BIG.view(bo + q * 8192, (8, 512), BF16) for q in range(2)]
                wdn = [BIG.view(bo + 16384 + q * 8192, (8, 512), BF16) for q in range(2)]
                for q in range(2):
                    wdma(wup[q], w_up_d[l, :, g * 1024 + q * 512:g * 1024 + (q + 1) * 512])
                for q in range(2):
                    wdma(wdn[q], w_dn_d[l, g * 1024:(g + 1) * 1024, q * 512:(q + 1) * 512])
                for s in range(NS):
                    sl = slice(s * 512, (s + 1) * 512)
                    u = ubuf[(g * 4 + s) % 2]
                    for f in range(8):
                        pu = psb().view(0, (512,), F32)
                        for k in range(8):
                            mm(pu, wup[f // 4][:, k, (f % 4) * 128:(f % 4 + 1) * 128], hT2[:, k, sl], k == 0, k == 7)
                        sq = sqs[rot['sq'] % 3]
                        rot['sq'] += 1
                        act(sq, pu, AF.Square)
                        stt(u[:, f], pu, 0.0, sq, ALU.is_gt, ALU.mult)
                    for j in range(8):
                        pd = psb().view(0, (512,), F32)
                        for f in range(8):
                            mm(pd, wdn[j // 4][:, f, (j % 4) * 128:(j % 4 + 1) * 128], u[:, f], f == 0, f == 7)
                        stt(xT[:, j, sl], pd, gt2[:, j:j + 1], xT[:, j, sl], ALU.mult, ALU.add)
                    for _ in range(2 if (g * 4 + s) < 8 else 1):
                        if nxt:
                            nxt.pop(0)()
            while nxt:
                nxt.pop(0)()
            gen_banks[0] = [0, 1, 2, 3, 4, 5, 6, 7]

        for l in layers:
            if stop != 'load':
                layer(l)

        S.mark('out')
        xo = BIG.view(0, (2, 8, 128), F32)
        outkeys = []
        for t in range(NT):
            xi = xo[:, t % 2]
            for hb in range(2):
                pv = psb().view(0, (4, 128), F32)
                for kk in range(4):
                    k = hb * 4 + kk
                    tr(pv[:, kk], xT[:, k, t * 128:(t + 1) * 128], identf)
                cp(xi[:, hb * 4:hb * 4 + 4], pv, eng='act' if hb else 'dve')
            ok = 'out%d' % t
            outkeys.append(ok)
            dma(Raw(out_d[t * 128:(t + 1) * 128, :].rearrange("p (a b) -> p a b", b=128), [ok]), xi, 'sp', key='xo%d' % (t % 2))

        S.emit(nc, st, final_wait_keys=outkeys + ['dbg_' + n for n in dbg_out])
    nc._marks = S.marks
    return nc, dbg_out


FUSED = True
_cache = {}


def _prog(layers):
    key = tuple(layers)
    if key not in _cache:
        _cache[key] = build_program(list(layers))[0]
    return _cache[key]


def _in_maps(x, c, positions, rms_mix, rms_mlp, b_ada, q_norm_swa, k_norm_swa, q_norm_moba, k_norm_moba,
             swa_sinks, weights):
    cb, cf = _host_consts()
    L = 2
    vecs = np.zeros((128, L * 68), np.float32)
    pidx = np.arange(128) % 64
    for l in range(L):
        o = l * 68
        vecs[:, o + 0:o + 8] = rms_mix[l].reshape(8, 128).T
        vecs[:, o + 8:o + 16] = rms_mlp[l].reshape(8, 128).T
        vecs[:, o + 16:o + 64] = b_ada[l].reshape(48, 128).T
        vecs[:, o + 64] = q_norm_swa[l][pidx]
        vecs[:, o + 65] = k_norm_swa[l][pidx]
        vecs[:, o + 66] = q_norm_moba[l][pidx]
        vecs[:, o + 67] = k_norm_moba[l][pidx]
    maps = []
    for b in range(x.shape[0]):
        m = dict(x=np.ascontiguousarray(x[b]), pos=np.ascontiguousarray(positions[b]).astype(np.int32),
                 cT=np.ascontiguousarray(c[b].reshape(8, 128).T), vecs=vecs,
                 sinks=np.ascontiguousarray(swa_sinks), cb=cb, cf=cf)
        m.update(weights)
        maps.append(m)
    return maps


def kernel(x, c, positions, rms_mix, rms_mlp, w_ada, b_ada, w_in, q_norm_swa, k_norm_swa,
           q_norm_moba, k_norm_moba, swa_sinks, w_o_swa, w_o_moba, w_out, w_up, w_down):
    f = lambda a: np.ascontiguousarray(np.asarray(a, dtype=np.float32))
    x = f(x)
    weights = dict(w_ada=f(w_ada), w_in=f(w_in), w_o_swa=f(w_o_swa), w_o_moba=f(w_o_moba),
                   w_out=f(w_out), w_up=f(w_up), w_down=f(w_down))
    args = (f(c), np.asarray(positions), f(rms_mix), f(rms_mlp), f(b_ada), f(q_norm_swa), f(k_norm_swa),
            f(q_norm_moba), f(k_norm_moba), f(swa_sinks), weights)
    n = x.shape[0]
    stages = [[0, 1]] if FUSED else [[0], [1]]
    cur = x
    for layers in stages:
        nc = _prog(layers)
        maps = _in_maps(cur, *args)
        res = run_bass_kernel_spmd(nc, maps, core_ids=list(range(n)))
        cur = np.stack([np.asarray(r["out"], dtype=np.float32) for r in res.results], axis=0)
    return cur
```

```python
import numpy as np
from contextlib import ExitStack
import concourse.bass as bass
import concourse.mybir as mybir
from concourse.bass_utils import run_bass_kernel_spmd

F32 = mybir.dt.float32
BF16 = mybir.dt.bfloat16
I32 = mybir.dt.int32
AF = mybir.ActivationFunctionType
ALU = mybir.AluOpType
AX = mybir.AxisListType

D = 1024
SEQ = 2048
NT = 16
NS = 4
DFF = 4096
INW = 4352
EPS = 1e-6
NEG = -1.0e5
SCALE = 0.125
DSIZE = {F32: 4, BF16: 2, I32: 4}


class Sched:
    def __init__(self):
        self.ops = []
        self.last_writer = {}
        self.readers = {}
        self.last_dma = {}
        self.marks = []

    def mark(self, label):
        self.marks.append((label, sum(1 for o in self.ops if o['eng'] == 'pe')))

    def add(self, eng, fn, reads=(), writes=(), dma_key=None):
        idx = len(self.ops)
        deps = set()
        for k in reads:
            w = self.last_writer.get(k)
            if w is not None:
                deps.add(w)
        raw = set(deps)
        for k in writes:
            w = self.last_writer.get(k)
            if w is not None:
                deps.add(w)
            for r in self.readers.get(k, ()):
                deps.add(r)
        for k in reads:
            self.readers.setdefault(k, []).append(idx)
        for k in writes:
            self.last_writer[k] = idx
            self.readers[k] = []
        if dma_key is not None:
            prev = self.last_dma.get(dma_key)
            if prev is not None:
                deps.add(prev)
            self.last_dma[dma_key] = idx
        deps.discard(idx)
        self.ops.append(dict(eng=eng, fn=fn, deps=deps, raw=raw, dma_key=dma_key, signal=False))
        return idx

    def emit(self, nc, stack, final_wait_keys=()):
        ops = self.ops
        fin = set()
        for k in final_wait_keys:
            w = self.last_writer.get(k)
            if w is not None:
                fin.add(w)
        if fin:
            ops.append(dict(eng='sp', fn=None, deps=fin, raw=set(), dma_key=None, signal=False))

        def needs_edge(op, d):
            p = ops[d]
            if p['dma_key'] is not None:
                return True
            if p['eng'] != op['eng']:
                return True
            return p['eng'] != 'pe'

        for op in ops:
            for d in op['deps']:
                p = ops[d]
                if p['dma_key'] is None and needs_edge(op, d):
                    p['signal'] = True
        engs = ['pe', 'act', 'dve', 'pool', 'sp']
        cnt = {e: 0 for e in engs}
        dcnt = {}
        for op in ops:
            if op['dma_key'] is not None:
                dcnt[op['dma_key']] = dcnt.get(op['dma_key'], 0) + 16
                op['dmaval'] = dcnt[op['dma_key']]
            elif op['signal']:
                cnt[op['eng']] += 1
                op['sigval'] = cnt[op['eng']]
        sems = {e: stack.enter_context(nc.semaphore('s_' + e)) for e in engs if cnt[e] > 0}
        dsems = {k: stack.enter_context(nc.semaphore('d_%s' % k)) for k in dcnt}
        per_eng = {e: [] for e in engs}
        for i, op in enumerate(ops):
            per_eng[op['eng']].append(i)
        block = stack.enter_context(nc.Block())

        def run(engname, eng):
            waited = {}
            for i in per_eng[engname]:
                op = ops[i]
                need = {}
                for d in op['deps']:
                    p = ops[d]
                    if p['dma_key'] is not None:
                        s, v = dsems[p['dma_key']], p['dmaval']
                    elif needs_edge(op, d):
                        s, v = sems[p['eng']], p['sigval']
                    else:
                        continue
                    kk = id(s)
                    if v > need.get(kk, (None, 0))[1]:
                        need[kk] = (s, v)
                for kk, (s, v) in need.items():
                    if waited.get(kk, 0) >= v:
                        continue
                    eng.wait_ge(s, v)
                    waited[kk] = v
                if op['fn'] is None:
                    continue
                ins = op['fn'](eng)
                if op['dma_key'] is not None:
                    ins.then_inc(dsems[op['dma_key']], 16)
                elif op['signal']:
                    ins.then_inc(sems[engname], 1)

        @block.tensor
        def _(e):
            run('pe', e)

        @block.scalar
        def _(e):
            run('act', e)

        @block.vector
        def _(e):
            run('dve', e)

        @block.gpsimd
        def _(e):
            run('pool', e)

        @block.sync
        def _(e):
            run('sp', e)


class Arena:
    def __init__(self, nc, st, name, nbytes, gran=512):
        self.name = name
        self.gran = gran
        self.nbytes = nbytes
        self.t = {BF16: st.enter_context(nc.sbuf_tensor(name, [128, nbytes // 2], BF16))}
        self.t[F32] = self.t[BF16].bitcast(F32)
        self.t[I32] = self.t[BF16].bitcast(I32)

    def view(self, off, shape, dt):
        return View(self, off, tuple(shape), dt)


class PsumBank:
    def __init__(self, nc, st, name):
        self.name = name
        self.gran = 2048
        self.t = {F32: st.enter_context(nc.psum_tensor(name, [128, 512], F32))}
        self.t[BF16] = self.t[F32].bitcast(BF16)

    def view(self, off, shape, dt):
        return View(self, off, tuple(shape), dt)


class View:
    def __init__(self, arena, off, shape, dt, p0=0, p1=128, idx=None, bc=None):
        self.arena, self.off, self.shape, self.dt = arena, off, shape, dt
        self.p0, self.p1 = p0, p1
        self.idx = idx if idx is not None else tuple(slice(0, n) for n in shape)
        self.bc = bc

    def __getitem__(self, key):
        if not isinstance(key, tuple):
            key = (key,)
        pk = key[0]
        p0, p1 = self.p0, self.p1
        if isinstance(pk, slice) and pk != slice(None):
            p0, p1 = self.p0 + (pk.start or 0), self.p0 + pk.stop
        rest = list(key[1:])
        new = []
        ri = 0
        for cur in self.idx:
            if isinstance(cur, int):
                new.append(cur)
                continue
            if ri < len(rest):
                r = rest[ri]
                ri += 1
                if isinstance(r, int):
                    new.append(cur.start + r)
                else:
                    a = cur.start + (r.start or 0)
                    b = cur.start + (r.stop if r.stop is not None else cur.stop - cur.start)
                    new.append(slice(a, b))
            else:
                new.append(cur)
        return View(self.arena, self.off, self.shape, self.dt, p0, p1, tuple(new))

    def part(self, p0, p1):
        return View(self.arena, self.off, self.shape, self.dt, self.p0 + p0, self.p0 + p1, self.idx)

    @property
    def ap(self):
        esz = DSIZE[self.dt]
        n = int(np.prod(self.shape))
        e0 = self.off // esz
        base = self.arena.t[self.dt][self.p0:self.p1, e0:e0 + n]
        if len(self.shape) > 1:
            names = ' '.join('d%d' % i for i in range(len(self.shape)))
            kw = {'d%d' % i: self.shape[i] for i in range(1, len(self.shape))}
            base = base.rearrange('p (%s) -> p %s' % (names, names), **kw)
        return base[(slice(None),) + tuple(self.idx)]

    @property
    def keys(self):
        esz = DSIZE[self.dt]
        strides = []
        s = esz
        for nn in reversed(self.shape):
            strides.append(s)
            s *= nn
        strides = strides[::-1]
        lo = self.off
        hi = self.off
        for i, cur in enumerate(self.idx):
            if isinstance(cur, int):
                lo += cur * strides[i]
                hi += cur * strides[i]
            else:
                lo += cur.start * strides[i]
                hi += (cur.stop - 1) * strides[i]
        hi += esz
        g = self.arena.gran
        halves = []
        if self.p0 < 64:
            halves.append(0)
        if self.p1 > 64:
            halves.append(1)
        return [(self.arena.name, gi, h) for gi in range(lo // g, (hi - 1) // g + 1) for h in halves]


class Raw:
    def __init__(self, ap, keys=()):
        self.ap = ap
        self.keys = list(keys)


def _host_consts():
    cb = np.zeros((128, 1664), np.float32)
    eye = np.eye(128, dtype=np.float32)
    cb[:, 0:128] = eye
    rt = np.zeros((128, 128), np.float32)
    for m in range(128):
        d = m % 64
        if d < 32:
            rt[m + 32, m] = -1.0
        else:
            rt[m - 32, m] = 1.0
    cb[:, 128:256] = rt
    bones = np.zeros((128, 128), np.float32)
    bones[0:64, 0:64] = 1.0 / 64
    bones[64:128, 64:128] = 1.0 / 64
    cb[:, 256:384] = bones
    cb[:, 384:512] = 1.0 / 1024
    kk = np.arange(128)[:, None]
    qq = np.arange(128)[None, :]
    for r4 in range(4):
        cb[:, 512 + r4 * 128:512 + (r4 + 1) * 128] = np.where(kk <= qq, 0.0, NEG)
        cb[:, 1024 + r4 * 128:1024 + (r4 + 1) * 128] = np.where(kk > qq, 0.0, NEG)
    cb[:, 1536 + 64:1536 + 128] = 1.0
    cf = np.zeros((128, 128 + 8), np.float32)
    cf[:, 0:128] = eye
    p = np.arange(128)
    cf[:, 128] = (10000.0 ** (-((p % 32).astype(np.float32)) / 32.0)).astype(np.float32)
    cf[:, 129] = -np.pi
    cf[:, 130] = EPS
    return cb, cf


def build_program(layers, n_total_layers=2, dbg=None, stop=None):
    nc = bass.Bass("TRN2", target_bir_lowering=False)
    L = n_total_layers
    x_d = nc.dram_tensor("x", [SEQ, D], F32, kind="ExternalInput").ap()
    out_d = nc.dram_tensor("out", [SEQ, D], F32, kind="ExternalOutput").ap()
    pos_d = nc.dram_tensor("pos", [SEQ], I32, kind="ExternalInput").ap()
    cT_d = nc.dram_tensor("cT", [128, 8], F32, kind="ExternalInput").ap()
    vec_d = nc.dram_tensor("vecs", [128, L * 68], F32, kind="ExternalInput").ap()
    sink_d = nc.dram_tensor("sinks", [L, 8], F32, kind="ExternalInput").ap()
    cb_d = nc.dram_tensor("cb", [128, 1664], F32, kind="ExternalInput").ap()
    cf_d = nc.dram_tensor("cf", [128, 136], F32, kind="ExternalInput").ap()
    w_ada_d = nc.dram_tensor("w_ada", [L, D, 6 * D], F32, kind="ExternalInput").ap()
    w_in_d = nc.dram_tensor("w_in", [L, D, INW], F32, kind="ExternalInput").ap()
    w_oa_d = nc.dram_tensor("w_o_swa", [L, 512, D], F32, kind="ExternalInput").ap()
    w_ob_d = nc.dram_tensor("w_o_moba", [L, 512, D], F32, kind="ExternalInput").ap()
    w_out_d = nc.dram_tensor("w_out", [L, D, D], F32, kind="ExternalInput").ap()
    w_up_d = nc.dram_tensor("w_up", [L, D, DFF], F32, kind="ExternalInput").ap()
    w_dn_d = nc.dram_tensor("w_down", [L, DFF, D], F32, kind="ExternalInput").ap()

    S = Sched()
    dbg_out = {}
    with ExitStack() as st:
        XA = Arena(nc, st, "XA", 65536, gran=2048)
        AA = Arena(nc, st, "AA", 32768, gran=1024)
        BIG = Arena(nc, st, "BIG", 65536, gran=256)
        TAB = Arena(nc, st, "TAB", 12288, gran=1024)
        SB = Arena(nc, st, "SB", 16384, gran=256)
        SF = Arena(nc, st, "SF", 6144, gran=2048)
        SF2 = Arena(nc, st, "SF2", 4096, gran=1024)
        CB = Arena(nc, st, "CB", 1664 * 2, gran=256)
        CF = Arena(nc, st, "CF", 136 * 4 + 32, gran=4096)
        VEC = Arena(nc, st, "VEC", 4096, gran=64)
        PS = [PsumBank(nc, st, "ps%d" % i) for i in range(8)]

        xT = XA.view(0, (8, SEQ), F32)
        AT = AA.view(0, (8, SEQ), BF16)
        ang = TAB.view(0, (SEQ,), F32)
        cos_s = TAB.view(8192, (512,), F32)
        sin_s = TAB.view(10240, (512,), F32)
        hTs = SB.view(0, (8, 512), BF16)
        sqb = SB.view(8192, (512,), BF16)
        xnb = SB.view(9216, (512,), BF16)
        Pb = [SB.view(10240, (512,), BF16), SB.view(11264, (512,), BF16)]
        biasT = SB.view(12288, (512,), BF16)
        biasq = SB.view(13312, (64,), BF16)
        mixs = SB.view(8192, (8, 512), BF16)
        ubuf = [SB.view(0, (8, 512), BF16), SB.view(8192, (8, 512), BF16)]
        rstd = SF.view(0, (512,), F32)
        t1 = SF.view(2048, (512,), F32)
        t2 = SF.view(4096, (512,), F32)
        identb = CB.view(0, (128,), BF16)
        Rt = CB.view(256, (128,), BF16)
        Bones = CB.view(512, (128,), BF16)
        Oones = CB.view(768, (128,), BF16)
        triL4 = CB.view(1024, (512,), BF16)
        triU4 = CB.view(2048, (512,), BF16)
        triL = CB.view(1024, (128,), BF16)
        onespad = CB.view(3072, (128,), BF16)
        identf = CF.view(0, (128,), F32)
        invf = CF.view(512, (1,), F32)
        negpi = CF.view(516, (1,), F32)
        epsc = CF.view(520, (1,), F32)
        def vec(l, off, n, dt=F32):
            return VEC.view(l * 1536 + off, (n,), dt)
        silc = VEC.view(3072, (8,), BF16)
        cin = VEC.view(3104, (8,), F32)
        esink = VEC.view(3136, (16,), F32)
        kmean = VEC.view(3328, (4, 8), F32)
        kmpad = VEC.view(3456, (4, 64), BF16)
        gsb = VEC.view(3968, (8,), F32)

        psc = [0]

        def mm(out, lhsT, rhs, start, stop):
            r = list(lhsT.keys) + list(rhs.keys)
            if not start:
                r += list(out.keys)
            S.add('pe', lambda e, o=out.ap, l=lhsT.ap, rr=rhs.ap: e.matmul(o, l, rr, start=start, stop=stop),
                  reads=r, writes=out.keys)

        def tr(out, in_, ident):
            S.add('pe', lambda e, o=out.ap, i=in_.ap, d=ident.ap: e.transpose(o, i, d),
                  reads=list(in_.keys) + list(ident.keys), writes=out.keys)

        def act(out, in_, func, bias=None, scale=1.0, accum=None):
            r = list(in_.keys)
            kw = {}
            if bias is not None:
                if isinstance(bias, (View, Raw)):
                    r += list(bias.keys)
                    kw['bias'] = bias.ap
                else:
                    kw['bias'] = bias
            if isinstance(scale, (View, Raw)):
                r += list(scale.keys)
                kw['scale'] = scale.ap
            else:
                kw['scale'] = scale
            w = list(out.keys)
            if accum is not None:
                kw['accum_out'] = accum.ap
                w += list(accum.keys)
            S.add('act', lambda e, o=out.ap, i=in_.ap: e.activation(out=o, in_=i, func=func, **kw),
                  reads=r, writes=w)

        def tt(out, in0, in1, op, eng='dve'):
            S.add(eng, lambda e, o=out.ap, a=in0.ap, b=in1.ap: e.tensor_tensor(out=o, in0=a, in1=b, op=op),
                  reads=list(in0.keys) + list(in1.keys), writes=out.keys)

        def ts(out, in0, s1, s2, op0, op1=None, eng='dve'):
            r = list(in0.keys)
            a1 = s1
            a2 = s2
            if isinstance(s1, (View, Raw)):
                r += list(s1.keys)
                a1 = s1.ap
            if isinstance(s2, (View, Raw)):
                r += list(s2.keys)
                a2 = s2.ap
            if op1 is None:
                S.add(eng, lambda e, o=out.ap, a=in0.ap: e.tensor_scalar(out=o, in0=a, scalar1=a1, scalar2=None, op0=op0),
                      reads=r, writes=out.keys)
            else:
                S.add(eng, lambda e, o=out.ap, a=in0.ap: e.tensor_scalar(out=o, in0=a, scalar1=a1, scalar2=a2, op0=op0, op1=op1),
                      reads=r, writes=out.keys)

        def stt(out, in0, sc, in1, op0, op1, eng='dve'):
            r = list(in0.keys) + list(in1.keys)
            a = sc
            if isinstance(sc, (View, Raw)):
                r += list(sc.keys)
                a = sc.ap
            S.add(eng, lambda e, o=out.ap, x=in0.ap, y=in1.ap: e.scalar_tensor_tensor(out=o, in0=x, scalar=a, in1=y, op0=op0, op1=op1),
                  reads=r, writes=out.keys)

        def cp(out, in_, eng='dve'):
            if eng == 'act':
                S.add('act', lambda e, o=out.ap, i=in_.ap: e.copy(out=o, in_=i), reads=in_.keys, writes=out.keys)
            else:
                S.add(eng, lambda e, o=out.ap, i=in_.ap: e.tensor_copy(out=o, in_=i), reads=in_.keys, writes=out.keys)

        def recip(out, in_):
            S.add('dve', lambda e, o=out.ap, i=in_.ap: e.reciprocal(out=o, in_=i), reads=in_.keys, writes=out.keys)

        def memset(out, val, eng='dve'):
            S.add(eng, lambda e, o=out.ap: e.memset(o, val), writes=out.keys)

        dmac = [0]

        def dma(out, in_, eng='sp', key=None):
            if key is None:
                dmac[0] += 1
                key = '%s%d' % ('g' if eng == 'pool' else 'q', dmac[0] % 20)
            S.add(eng, lambda e, o=out.ap, i=in_.ap: e.dma_start(out=o, in_=i),
                  reads=in_.keys, writes=out.keys, dma_key=key)

        def dump(name, v, shape, dt):
            if dbg is None or name not in dbg:
                return
            d = nc.dram_tensor("dbg_" + name, list(shape), dt, kind="ExternalOutput").ap()
            dma(Raw(d, ['dbg_' + name]), v, 'sp', key='dbg')
            dbg_out[name] = 1

        gen_banks = [[0, 1, 2, 3, 4, 5, 6, 7]]

        def psb():
            lst = gen_banks[0]
            b = PS[lst[psc[0] % len(lst)]]
            psc[0] += 1
            return b

        dma(CB.view(0, (1664,), BF16), Raw(cb_d), 'pool')
        dma(CF.view(0, (136,), F32), Raw(cf_d), 'sp')
        for l in range(L):
            dma(vec(l, 0, 68), Raw(vec_d[:, l * 68:(l + 1) * 68]), 'sp')
        dma(cin, Raw(cT_d), 'sp')
        dma(esink, Raw(sink_d.rearrange("l h -> (l h)").partition_broadcast(128)), 'sp')
        act(silc, cin, AF.Silu)
        act(esink, esink, AF.Exp)

        xin = BIG.view(0, (4, D), F32)
        for t in range(NT):
            xi = xin[:, t % 4]
            dma(xi, Raw(x_d[t * 128:(t + 1) * 128, :]), 'sp', key='xin%d' % (t % 4))
            for hb in range(2):
                bank = psb()
                pv = bank.view(0, (4, 128), F32)
                for kk in range(4):
                    k = hb * 4 + kk
                    tr(pv[:, kk], xi[:, k * 128:(k + 1) * 128], identf)
                cp(xT[:, hb * 4:hb * 4 + 4, t * 128:(t + 1) * 128], pv, eng='act' if hb else 'dve')

        posi = BIG.view(16384, (SEQ,), I32)
        posf = BIG.view(24576, (SEQ,), F32)
        dma(posi, Raw(pos_d.partition_broadcast(128)), 'sp')
        cp(posf, posi)
        ts(ang, posf, invf, None, ALU.mult)

        C1 = 6.28125
        C2 = float(2 * np.pi - 6.28125)
        TWO_PI = float(2 * np.pi)

        def rope_tables(s, cs=None, C=None, Ci=None):
            sl = slice(s * 512, (s + 1) * 512)
            x = ang[:, sl]
            B_, A_ = cs if cs is not None else (cos_s, sin_s)
            C = t1 if C is None else C
            Ci = SF.view(4096, (512,), I32) if Ci is None else Ci
            ts(C, x, 1.0 / TWO_PI, None, ALU.mult)
            cp(Ci, C)
            cp(C, Ci)
            stt(A_, C, -C1, x, ALU.mult, ALU.add)
            stt(A_, C, -C2, A_, ALU.mult, ALU.add)
            ts(C, A_, float(np.pi), None, ALU.is_gt)
            stt(A_, C, -TWO_PI, A_, ALU.mult, ALU.add)
            ts(B_, A_, float(np.pi / 2), None, ALU.add)
            ts(C, B_, float(np.pi), None, ALU.is_gt)
            stt(B_, C, -TWO_PI, B_, ALU.mult, ALU.add)
            act(A_, A_, AF.Sin)
            act(B_, B_, AF.Sin)

        def mod_groups(l, part, bufs, bank=None):
            c0, c1 = (0, 16) if part == 0 else (16, 48)
            ngrp = (c1 - c0) // 2
            stt_ = {}
            fns = []
            for gi in range(ngrp):
                def fn(gi=gi):
                    if gi == 0:
                        stt_['pm'] = (psb() if bank is None else bank).view(0, (48,), F32)
                    pm = stt_['pm']
                    col0 = c0 + gi * 2
                    wA = bufs[gi % len(bufs)]
                    dma(wA, Raw(w_ada_d[l, :, col0 * 128:(col0 + 2) * 128].rearrange("(kc p) c -> p kc c", p=128)),
                        'pool', key='wA%d_%d' % (part, gi % len(bufs)))
                    for j in range(2):
                        for k in range(8):
                            mm(pm[:, col0 + j:col0 + j + 1], wA[:, k, j * 128:(j + 1) * 128], silc[:, k:k + 1],
                               start=(k == 0), stop=(k == 7))
                    if gi == ngrp - 1:
                        modv = vec(l, 272, 48)
                        tt(modv[:, c0:c1], pm[:, c0:c1], vec(l, 0, 68)[:, 16 + c0:16 + c1], ALU.add)
                        if part == 0:
                            stt(vec(l, 464, 8), modv[:, 8:16], 1.0, vec(l, 0, 68)[:, 0:8], ALU.add, ALU.mult)
                        else:
                            stt(vec(l, 496, 8), modv[:, 32:40], 1.0, vec(l, 0, 68)[:, 8:16], ALU.add, ALU.mult)
                fns.append(fn)
            return fns

        def A1(l):
            return vec(l, 464, 8)

        def A2(l):
            return vec(l, 496, 8)

        def MODV(l):
            return vec(l, 272, 48)

        def GAIN(l, i):
            return vec(l, 0, 68)[:, 64 + i:65 + i]

        def norm_tile(s, acol, bcol, dest):
            sl = slice(s * 512, (s + 1) * 512)
            bank = psb()
            pm = bank.view(0, (512,), F32)
            for k in range(8):
                act(dest[:, k], xT[:, k, sl], AF.Square)
                mm(pm, Oones, dest[:, k], start=(k == 0), stop=(k == 7))
            act(rstd, pm, AF.Ln, bias=epsc)
            act(rstd, rstd, AF.Exp, scale=-0.5)
            for k in range(8):
                stt(t1 if k % 2 == 0 else t2, xT[:, k, sl], acol[:, k:k + 1], rstd, ALU.mult, ALU.mult)
                act(dest[:, k], t1 if k % 2 == 0 else t2, AF.Identity, bias=bcol[:, k:k + 1])

        qksets = [(sqb, xnb, rstd, t1),
                  (SF2.view(2048, (512,), BF16), SF2.view(3072, (512,), BF16), t2, SF2.view(0, (512,), F32))]
        qkc = [0]
        pinc = [0]
        qkn = [2]
        pmbanks = [[3]]

        def qk_item(proj, gain, dest, cs=None):
            st_ = {}

            def s0():
                st_['pin'] = PS[pinc[0] % 3].view(0, (512,), F32)
                pinc[0] += 1
                st_['set'] = qksets[qkc[0] % qkn[0]]
                qkc[0] += 1
                proj(st_['pin'])

            def s1():
                sq_, xn_, r_, tt1_ = st_['set']
                pin = st_['pin']
                act(sq_, pin, AF.Square)
                pm = PS[pmbanks[0][qkc[0] % len(pmbanks[0])]].view(0, (512,), F32)
                mm(pm, Bones, sq_, True, True)
                act(r_, pm, AF.Ln, bias=epsc)
                act(r_, r_, AF.Exp, scale=-0.5)
                stt(xn_, pin, gain, r_, ALU.mult, ALU.mult)

            def s2():
                sq_, xn_, r_, tt1_ = st_['set']
                pr = PS[4 + st_.setdefault('prb', qkc[0] % 2)].view(0, (512,), F32)
                mm(pr, Rt, xn_, True, True)
                cos_, sin_ = cs if cs is not None else (cos_s, sin_s)
                tt(tt1_, xn_, cos_, ALU.mult)
                tt(r_, pr, sin_, ALU.mult)
                tt(dest, tt1_, r_, ALU.add)
            return (s0, s1, s2)

        def run_items(items, fillers=None, after_proj=None, mid=None):
            n = len(items)
            for i in range(n + 2):
                if i == 3 and mid is not None:
                    mid()
                if fillers:
                    fillers.pop(0)()
                if i < n:
                    items[i][0]()
                if i == n - 1 and after_proj is not None:
                    after_proj()
                if 0 <= i - 1 < n and items[i - 1][1] is not None:
                    items[i - 1][1]()
                if 0 <= i - 2 < n and items[i - 2][2] is not None:
                    items[i - 2][2]()

        BIGROW = 32768
        AAROW = 16384
        CBROW = 896
        VECROW = 2048
        rot = {'sc': 0, 'po': 0, 'pb': 0, 'sq': 0, 'va': 0, 'nz': 0}

        def wdma(dst, src2d):
            dma(dst, Raw(src2d.rearrange("(kc p) c -> p kc c", p=128)), 'pool')

        def layer(l):
            S.mark('L%d mod' % l)
            if l == layers[0]:
                for fn in mod_groups(l, 0, [BIG.view(32768, (8, 256), BF16), BIG.view(36864, (8, 256), BF16)]):
                    fn()
                modfill = mod_groups(l, 1, [AA.view(24576, (8, 256), BF16), AA.view(28672, (8, 256), BF16)], PS[7])
            else:
                modfill = []
            if stop == 'mod':
                return
            modv = MODV(l)
            a1, a2 = A1(l), A2(l)
            sh1, gt1, sh2, gt2 = modv[:, 0:8], modv[:, 16:24], modv[:, 24:32], modv[:, 40:48]
            kaT = BIG.view(0, (2, SEQ), BF16)
            kbT = BIG.view(8192, (4, SEQ), BF16)
            kbT4 = BIG.view(8192, (4, 8, 256), BF16)
            VOFF = 24576
            vst = BIG.view(VOFF, (16, 640), BF16)
            wqa = BIG.view(45312, (8, 512), BF16)
            wqb = BIG.view(45312 + 8192, (8, 512), BF16)
            wkva = AA.view(0, (8, 256), BF16)
            wkb = AA.view(4096, (8, 512), BF16)
            wvb = AA.view(12288, (8, 512), BF16)
            wkad = AA.view(20480, (8, 2, 2, 64), BF16)
            wdma(wkva, w_in_d[l, :, 512:768])
            for g in range(2):
                for dd in range(2):
                    wdma(wkad[:, :, g, dd], w_in_d[l, :, 512 + g * 64:512 + (g + 1) * 64])
            wdma(wkb, w_in_d[l, :, 1280:1792])
            wdma(wvb, w_in_d[l, :, 1792:2304])
            wdma(wqa, w_in_d[l, :, 0:512])
            csA = [(cos_s, sin_s), (BIG.view(53504, (512,), F32), BIG.view(55552, (512,), F32))]
            ropeC, ropeCi = BIG.view(57600, (512,), F32), BIG.view(59648, (512,), I32)

            vflat = BIG.view(VOFF, (16 * 640 + 64,), BF16)
            memset(vflat[:, 16 * 640:16 * 640 + 64], 0.0)

            def vprep(c, col):
                va = vaug[rot['va'] % 4]
                rot['va'] += 1
                cp(va[:, 0:64], vst[:, c, col:col + 64], eng='pool')
                return va

            S.mark('L%d A' % l)
            norm_tile(0, a1, sh1, hTs)
            rope_tables(0, csA[0], ropeC, ropeCi)
            for s in range(NS):
                sl = slice(s * 512, (s + 1) * 512)
                gen_banks[0] = [6] if modfill else [6, 7]
                items = []
                wkad2 = AA.view(20480, (8, 2, 128), BF16)

                def mk_ka(g):
                    def proj(pin):
                        for k in range(8):
                            mm(pin, wkad2[:, k, g], hTs[:, k], k == 0, k == 7)
                    return qk_item(proj, GAIN(l, 1), kaT[:, g, sl], csA[s % 2])

                def mk_kb(p):
                    def proj(pin):
                        for k in range(8):
                            mm(pin, wkb[:, k, p * 128:(p + 1) * 128], hTs[:, k], k == 0, k == 7)
                    return qk_item(proj, GAIN(l, 3), kbT[:, p, sl], csA[s % 2])

                def mk_v(tq):
                    def s0():
                        c = s * 4 + tq
                        tsl = slice(tq * 128, (tq + 1) * 128)
                        pva = psb().view(0, (128,), F32)
                        for k in range(8):
                            mm(pva, hTs[:, k, tsl], wkva[:, k, 128:256], k == 0, k == 7)
                        cp(vst[:, c, 0:128], pva, eng='act')
                        pvb = psb().view(0, (512,), F32)
                        for k in range(8):
                            mm(pvb, hTs[:, k, tsl], wvb[:, k], k == 0, k == 7)
                        cp(vst[:, c, 128:640], pvb, eng='dve')
                    return (s0, None, None)
                kitems = [mk_ka(0), mk_ka(1)] + [mk_kb(p) for p in range(4)]
                vitems = [mk_v(tq) for tq in range(4)]
                items = [vitems[0], vitems[1], kitems[0], vitems[2], kitems[1], vitems[3]] + kitems[2:]
                run_items(items, modfill,
                          after_proj=(lambda s=s: norm_tile(s + 1, a1, sh1, hTs)) if s + 1 < NS else None,
                          mid=(lambda s=s: rope_tables(s + 1, csA[(s + 1) % 2], ropeC, ropeCi)) if s + 1 < NS else None)
            while modfill:
                modfill.pop(0)()
            gen_banks[0] = [0, 1, 2, 3, 4, 5, 6, 7]
            if stop == 'A':
                return
            dump('kaT', kaT, (128, 2, SEQ), BF16)
            dump('kbT', kbT, (128, 4, SEQ), BF16)
            dump('vst', vst, (128, 16, 640), BF16)

            qkn[0] = 2
            wdma(wqb, w_in_d[l, :, 768:1280])
            S.add('dve', lambda e: e.tensor_reduce(out=kmean.ap, in_=kbT4.ap, axis=AX.X, op=ALU.add),
                  reads=kbT4.keys, writes=kmean.keys)
            memset(kmpad, 0.0)
            for p in range(4):
                for hf in range(2):
                    h = 2 * p + hf
                    ts(kmpad[hf * 64:(hf + 1) * 64, p, h * 8:(h + 1) * 8], kmean[hf * 64:(hf + 1) * 64, p],
                       1.0 / 256, None, ALU.mult)

            biasq128 = SB.view(13312, (128,), BF16)
            memset(biasq128, 0.0)
            memset(biasT, 0.0)
            qz = [[BIG.view(61696, (512,), BF16), BIG.view(62720, (512,), BF16)],
                  [SB.view(13568, (512,), BF16), SB.view(14592, (512,), BF16)]]
            for a_ in range(2):
                for b_ in range(2):
                    memset(qz[a_][b_], 0.0)
            qrot = [0, 0]
            gs3 = VEC.view(1024, (8, 8), F32)
            biasq3 = SB.view(13312, (8, 8), BF16)
            E_GS = 1024 // 4
            VECROWF = 1024
            Pb3 = [Pb[0], Pb[1], BIG.view(63744, (512,), BF16), SF2.view(0, (512,), BF16)]
            vaug = [BIG.view(64768 + 256 * i_, (128,), BF16) for i_ in range(3)] + [SB.view(15616, (128,), BF16)]
            for i_ in range(4):
                memset(vaug[i_][:, 64:128], 1.0)
            norm_tile(0, a1, sh1, hTs)
            rope_tables(0)
            for s in range(NS):
                gen_banks[0] = [6]
                pmbanks[0] = [3, 7]
                S.mark('L%d B%d qproj' % (l, s))
                sl = slice(s * 512, (s + 1) * 512)
                def mk_q(p):
                    w = wqa if p < 4 else wqb
                    pc = p % 4

                    def proj(pin):
                        for k in range(8):
                            mm(pin, w[:, k, pc * 128:(pc + 1) * 128], hTs[:, k], k == 0, k == 7)
                    return qk_item(proj, GAIN(l, 0 if p < 4 else 2), AT[:, p, sl])
                run_items([mk_q(p) for p in range(8)],
                          after_proj=(lambda s=s: norm_tile(s + 1, a1, sh1, hTs)) if s + 1 < NS else None)
                if s + 1 < NS:
                    rope_tables(s + 1)
                if s == 0:
                    dump('qT0', AT[:, :, 0:512], (128, 8, 512), BF16)
                if stop == 'Bq':
                    return
                gen_banks[0] = [0]
                pmbanks[0] = [3]
                S.mark('L%d B%d bias' % (l, s))
                for tq in range(4):
                    i = s * 4 + tq
                    qb = i // 2
                    qsl = slice(s * 512 + tq * 128, s * 512 + (tq + 1) * 128)
                    memset(biasq3, NEG)
                    if qb < 4:
                        memset(biasq3[:, :, 0:qb + 1], 0.0)
                    else:
                        nb = qb
                        pgb = psb()
                        pg = pgb.view(0, (64,), F32)
                        for p4 in range(4):
                            mm(pg, AT[:, 4 + p4, qsl], kmpad[:, p4], p4 == 0, p4 == 3)
                        cp(gs3, pgb.view(0, (8, 8), F32))
                        cmpv = SF.view(2048, (8, nb, nb), F32)
                        rkv = SF.view(4096, (8, nb), F32)
                        gb1 = Raw(bass.AP(VEC.t[F32], E_GS, [[VECROWF, 128], [8, 8], [0, nb], [1, nb]]), gs3.keys)
                        gb0 = Raw(bass.AP(VEC.t[F32], E_GS, [[VECROWF, 128], [8, 8], [1, nb], [0, nb]]), gs3.keys)
                        tt(cmpv, gb1, gb0, ALU.is_gt)
                        S.add('dve', lambda e, o=rkv.ap, i_=cmpv.ap: e.tensor_reduce(out=o, in_=i_, axis=AX.X, op=ALU.add),
                              reads=cmpv.keys, writes=rkv.keys)
                        ts(rkv, rkv, -2.0, 0.0, ALU.add, ALU.max)
                        ts(biasq3[:, :, 0:nb], rkv, NEG, None, ALU.mult)
                        memset(biasq3[:, :, nb:nb + 1], 0.0)
                        if i == 8:
                            dump('gs8', gs3, (128, 8, 8), F32)
                            dump('bq8', biasq3, (128, 8, 8), BF16)
                            dump('rk8', rkv, (128, 8, nb), F32)
                    pt = psb().view(0, (128,), BF16)
                    tr(pt, biasq128, identb)
                    cp(biasT[0:64, tq * 128:(tq + 1) * 128], pt.part(0, 64))
                if stop == 'Bb':
                    return
                S.mark('L%d B%d attn' % (l, s))
                units = []

                def swa_units(g, tq):
                    i = s * 4 + tq
                    qsl = slice(s * 512 + tq * 128, s * 512 + (tq + 1) * 128)
                    chunks = [i - 1, i] if i > 0 else [i]
                    po = PS[5 + rot['po'] % 3].view(0, (512,), F32)
                    pz = po
                    rot['po'] += 1
                    qzb = [qz[0][qrot[0] % 2], qz[1][qrot[1] % 2]]
                    qrot[0] += 1
                    qrot[1] += 1
                    for ci, c in enumerate(chunks):
                        sc = PS[1 + rot['sc'] % 4].view(0, (512,), F32)
                        rot['sc'] += 1
                        pb = Pb3[rot['pb'] % 4]
                        rot['pb'] += 1
                        triap = triL4 if c == i else triU4

                        vh = {}

                        def score(sc=sc, c=c, triap=triap, qzb=qzb, ci=ci, vh=vh):
                            vh['va'] = vprep(c, g * 64)
                            if ci == 0:
                                for j in range(4):
                                    h = 4 * g + j
                                    hb = (h % 2) * 64
                                    cp(qzb[h % 2][hb:hb + 64, j * 128:(j + 1) * 128], AT[hb:hb + 64, h // 2, qsl])
                            mm(sc, identb, triap, True, False)
                            for j in range(4):
                                h = 4 * g + j
                                mm(sc[:, j * 128:(j + 1) * 128], kaT[:, g, c * 128:(c + 1) * 128],
                                   qzb[h % 2][:, j * 128:(j + 1) * 128], False, j == 3)

                        def ex(sc=sc, pb=pb):
                            act(pb, sc, AF.Exp, scale=SCALE)

                        last = (ci == len(chunks) - 1)

                        def pv(po=po, pb=pb, ci=ci, last=last, vh=vh):
                            mm(po, vh['va'], pb, ci == 0, last)

                        def post(pz=po, po=po, last=last):
                            if last:
                                rstd = [SF.view(0, (512,), F32), SF.view(2048, (512,), F32), SF.view(4096, (512,), F32)][rot['nz'] % 3]
                                rot['nz'] += 1
                                for j in range(4):
                                    h = 4 * g + j
                                    ts(rstd[64:128, j * 128:(j + 1) * 128], pz[64:128, j * 128:(j + 1) * 128],
                                       esink[64:128, l * 8 + h:l * 8 + h + 1], None, ALU.add)
                                act(rstd.part(64, 128), rstd.part(64, 128), AF.Ln)
                                act(rstd.part(64, 128), rstd.part(64, 128), AF.Exp, scale=-1.0)
                                for j in range(4):
                                    h = 4 * g + j
                                    hb = (h % 2) * 64
                                    tt(AT[hb:hb + 64, h // 2, qsl], po[0:64, j * 128:(j + 1) * 128],
                                       rstd[64:128, j * 128:(j + 1) * 128], ALU.mult)
                        units.append((score, ex, pv, post))

                def moba_units(h):
                    hb = (h % 2) * 64
                    pm_ = h // 2
                    pr_ = 4 + pm_
                    po = PS[5 + rot['po'] % 3].view(0, (512,), F32)
                    pz = po
                    rot['po'] += 1
                    qzh = qz[h % 2][qrot[h % 2] % 2]
                    qrot[h % 2] += 1
                    nch = 4 * s + 4
                    for j in range(nch):
                        cl = j - 4 * s
                        q0 = 0 if cl < 0 else cl * 128
                        b = j // 2
                        sc = PS[1 + rot['sc'] % 4].view(0, (512,), F32)
                        rot['sc'] += 1
                        pb = Pb3[rot['pb'] % 4]
                        rot['pb'] += 1
                        r = h * 8 + b
                        sel = Raw(identb[:, r:r + 1].ap.to_broadcast([128, 128]), identb.keys)

                        vh = {}

                        def score(sc=sc, j=j, cl=cl, q0=q0, sel=sel, vh=vh):
                            vh['va'] = vprep(j, 128 + h * 64)
                            if j == 0:
                                cp(qzh[hb:hb + 64], AT[hb:hb + 64, pr_, sl])
                            need_bias = (s >= 2) and (cl < 2)
                            if need_bias:
                                mm(sc[:, q0:512], sel, biasT[:, q0:512], True, False)
                            mm(sc[:, q0:512], kbT[:, pm_, j * 128:(j + 1) * 128], qzh[:, q0:512], not need_bias, cl < 0)
                            if cl >= 0:
                                mm(sc[:, q0:q0 + 128], identb, triL, False, True)

                        def ex(sc=sc, pb=pb, q0=q0):
                            act(pb[:, q0:512], sc[:, q0:512], AF.Exp, scale=SCALE)

                        def pv(po=po, pb=pb, j=j, q0=q0, vh=vh):
                            mm(po[:, q0:512], vh['va'], pb[:, q0:512], j == 0, j == nch - 1)

                        def post(pz=po, po=po, j=j):
                            if j == nch - 1:
                                rstd = [SF.view(0, (512,), F32), SF.view(2048, (512,), F32), SF.view(4096, (512,), F32)][rot['nz'] % 3]
                                rot['nz'] += 1
                                act(rstd.part(64, 128), pz.part(64, 128), AF.Ln)
                                act(rstd.part(64, 128), rstd.part(64, 128), AF.Exp, scale=-1.0)
                                tt(AT[hb:hb + 64, pr_, sl], po[0:64], rstd[64:128], ALU.mult)
                        units.append((score, ex, pv, post))

                for g in range(2):
                    for tq in range(4):
                        swa_units(g, tq)
                if stop not in ('Bs', 'Bs1', 'Bs2', 'Bs0', 'BsX'):
                    for h in range(8):
                        moba_units(h)
                SK = 3
                for ui in range(min(SK, len(units))):
                    units[ui][0]()
                for ui in range(len(units) + 2):
                    if ui + SK < len(units):
                        units[ui + SK][0]()
                    if ui < len(units):
                        units[ui][1]()
                        units[ui][2]()
                    if 0 <= ui - 2 < len(units):
                        units[ui - 2][3]()
                if stop in ('Bs', 'Bm', 'Bs1', 'Bs2', 'Bs0', 'BsX'):
                    return
            gen_banks[0] = [0, 1, 2, 3, 4, 5, 6, 7]
            if stop == 'B':
                return
            dump('AT', AT, (128, 8, SEQ), BF16)

            S.mark('L%d C' % l)
            wg = [BIG.view(q * 8192, (8, 512), BF16) for q in range(4)]
            woa = [BIG.view(32768 + q * 4096, (4, 512), BF16) for q in range(2)]
            wob = [BIG.view(40960 + q * 4096, (4, 512), BF16) for q in range(2)]
            wout = [BIG.view(49152 + q * 8192, (8, 512), BF16) for q in range(2)]
            for q in range(2):
                wdma(wg[q], w_in_d[l, :, 2304 + q * 512:2304 + (q + 1) * 512])
                wdma(wg[2 + q], w_in_d[l, :, 2304 + (2 + q) * 512:2304 + (3 + q) * 512])
                wdma(woa[q], w_oa_d[l, :, q * 512:(q + 1) * 512])
                wdma(wob[q], w_ob_d[l, :, q * 512:(q + 1) * 512])
            for q in range(2):
                wdma(wout[q], w_out_d[l, :, q * 512:(q + 1) * 512])
            norm_tile(0, a1, sh1, hTs)
            for s in range(NS):
                sl = slice(s * 512, (s + 1) * 512)
                for j in range(8):
                    qd, jc = j // 4, (j % 4) * 128
                    p1 = psb().view(0, (512,), F32)
                    for k in range(8):
                        mm(p1, wg[qd][:, k, jc:jc + 128], hTs[:, k], k == 0, k == 7)
                    p2 = psb().view(0, (512,), F32)
                    for k in range(8):
                        mm(p2, wg[2 + qd][:, k, jc:jc + 128], hTs[:, k], k == 0, k == 7)
                    p3 = psb().view(0, (512,), F32)
                    for k in range(4):
                        mm(p3, woa[qd][:, k, jc:jc + 128], AT[:, k, sl], k == 0, k == 3)
                    p4 = psb().view(0, (512,), F32)
                    for k in range(4):
                        mm(p4, wob[qd][:, k, jc:jc + 128], AT[:, 4 + k, sl], k == 0, k == 3)
                    act(rstd, p1, AF.Sigmoid)
                    act(t2, p2, AF.Sigmoid)
                    tt(t1, rstd, p3, ALU.mult)
                    tt(t2, t2, p4, ALU.mult)
                    tt(mixs[:, j], t1, t2, ALU.add)
                if s + 1 < NS:
                    norm_tile(s + 1, a1, sh1, hTs)
                for j in range(8):
                    qd, jc = j // 4, (j % 4) * 128
                    p5 = psb().view(0, (512,), F32)
                    for k in range(8):
                        mm(p5, wout[qd][:, k, jc:jc + 128], mixs[:, k], k == 0, k == 7)
                    stt(xT[:, j, sl], p5, gt1[:, j:j + 1], xT[:, j, sl], ALU.mult, ALU.add)
            if stop == 'C':
                return
            dump('xmid', xT, (128, 8, SEQ), F32)

            S.mark('L%d MLPnorm' % l)
            hT2 = AA.view(0, (8, SEQ), BF16)
            for s in range(NS):
                norm_tile(s, a2, sh2, hT2[:, :, s * 512:(s + 1) * 512])
            sqs = [rstd, t1, t2]
            nxt = []
            if l + 1 < L and (l + 1) in layers:
                mb = [TAB.view(8192, (8, 256), BF16), SF2.view(0, (8, 256), BF16)]
                nxt = mod_groups(l + 1, 0, mb, PS[7]) + mod_groups(l + 1, 1, mb, PS[7])
                gen_banks[0] = [0, 1, 2, 3, 4, 5, 6]
            for g in range(4):
                S.mark('L%d MLP g%d' % (l, g))
                bo = (g % 2) * 32768
                wup = [BIG.view(bo + q * 8192, (8, 512), BF16) for q in range(2)]
                wdn = [BIG.view(bo + 16384 + q * 8192, (8, 512), BF16) for q in range(2)]
                for q in range(2):
                    wdma(wup[q], w_up_d[l, :, g * 1024 + q * 512:g * 1024 + (q + 1) * 512])
                for q in range(2):
                    wdma(wdn[q], w_dn_d[l, g * 1024:(g + 1) * 1024, q * 512:(q + 1) * 512])
                for s in range(NS):
                    sl = slice(s * 512, (s + 1) * 512)
                    u = ubuf[(g * 4 + s) % 2]
                    for f in range(8):
                        pu = psb().view(0, (512,), F32)
                        for k in range(8):
                            mm(pu, wup[f // 4][:, k, (f % 4) * 128:(f % 4 + 1) * 128], hT2[:, k, sl], k == 0, k == 7)
                        sq = sqs[rot['sq'] % 3]
                        rot['sq'] += 1
                        act(sq, pu, AF.Square)
                        stt(u[:, f], pu, 0.0, sq, ALU.is_gt, ALU.mult)
                    for j in range(8):
                        pd = psb().view(0, (512,), F32)
                        for f in range(8):
                            mm(pd, wdn[j // 4][:, f, (j % 4) * 128:(j % 4 + 1) * 128], u[:, f], f == 0, f == 7)
                        stt(xT[:, j, sl], pd, gt2[:, j:j + 1], xT[:, j, sl], ALU.mult, ALU.add)
                    for _ in range(2 if (g * 4 + s) < 8 else 1):
                        if nxt:
                            nxt.pop(0)()
            while nxt:
                nxt.pop(0)()
            gen_banks[0] = [0, 1, 2, 3, 4, 5, 6, 7]

        for l in layers:
            if stop != 'load':
                layer(l)

        S.mark('out')
        xo = BIG.view(0, (4, 8, 128), F32)
        outkeys = []
        for t in range(NT):
            xi = xo[:, t % 4]
            for hb in range(2):
                pv = psb().view(0, (4, 128), F32)
                for kk in range(4):
                    k = hb * 4 + kk
                    tr(pv[:, kk], xT[:, k, t * 128:(t + 1) * 128], identf)
                cp(xi[:, hb * 4:hb * 4 + 4], pv, eng='act' if hb else 'dve')
            ok = 'out%d' % t
            outkeys.append(ok)
            dma(Raw(out_d[t * 128:(t + 1) * 128, :].rearrange("p (a b) -> p a b", b=128), [ok]), xi, 'sp', key='xo%d' % (t % 4))

        S.emit(nc, st, final_wait_keys=outkeys + ['dbg_' + n for n in dbg_out])
    nc._marks = S.marks
    return nc, dbg_out


FUSED = True
_cache = {}


def _prog(layers):
    key = tuple(layers)
    if key not in _cache:
        _cache[key] = build_program(list(layers))[0]
    return _cache[key]


def _in_maps(x, c, positions, rms_mix, rms_mlp, b_ada, q_norm_swa, k_norm_swa, q_norm_moba, k_norm_moba,
             swa_sinks, weights):
    cb, cf = _host_consts()
    L = 2
    vecs = np.zeros((128, L * 68), np.float32)
    pidx = np.arange(128) % 64
    for l in range(L):
        o = l * 68
        vecs[:, o + 0:o + 8] = rms_mix[l].reshape(8, 128).T
        vecs[:, o + 8:o + 16] = rms_mlp[l].reshape(8, 128).T
        vecs[:, o + 16:o + 64] = b_ada[l].reshape(48, 128).T
        vecs[:, o + 64] = q_norm_swa[l][pidx]
        vecs[:, o + 65] = k_norm_swa[l][pidx]
        vecs[:, o + 66] = q_norm_moba[l][pidx]
        vecs[:, o + 67] = k_norm_moba[l][pidx]
    maps = []
    for b in range(x.shape[0]):
        m = dict(x=np.ascontiguousarray(x[b]), pos=np.ascontiguousarray(positions[b]).astype(np.int32),
                 cT=np.ascontiguousarray(c[b].reshape(8, 128).T), vecs=vecs,
                 sinks=np.ascontiguousarray(swa_sinks), cb=cb, cf=cf)
        m.update(weights)
        maps.append(m)
    return maps


def kernel(x, c, positions, rms_mix, rms_mlp, w_ada, b_ada, w_in, q_norm_swa, k_norm_swa,
           q_norm_moba, k_norm_moba, swa_sinks, w_o_swa, w_o_moba, w_out, w_up, w_down):
    f = lambda a: np.ascontiguousarray(np.asarray(a, dtype=np.float32))
    x = f(x)
    weights = dict(w_ada=f(w_ada), w_in=f(w_in), w_o_swa=f(w_o_swa), w_o_moba=f(w_o_moba),
                   w_out=f(w_out), w_up=f(w_up), w_down=f(w_down))
    args = (f(c), np.asarray(positions), f(rms_mix), f(rms_mlp), f(b_ada), f(q_norm_swa), f(k_norm_swa),
            f(q_norm_moba), f(k_norm_moba), f(swa_sinks), weights)
    n = x.shape[0]
    stages = [[0, 1]] if FUSED else [[0], [1]]
    cur = x
    for layers in stages:
        nc = _prog(layers)
        maps = _in_maps(cur, *args)
        res = run_bass_kernel_spmd(nc, maps, core_ids=list(range(n)))
        cur = np.stack([np.asarray(r["out"], dtype=np.float32) for r in res.results], axis=0)
    return cur
```
